# Optimizing a Trainium2 kernel written in Bass

```python
import jax, jax.numpy as jnp
from jax import lax
import numpy as np

D_MODEL = 1024
BATCH = 4
SEQ = 8192
DEPTH = 2

SSD_HEADS = 16
SSD_HEAD_DIM = 64
SSD_INNER = SSD_HEADS * SSD_HEAD_DIM
SSD_GROUPS = 2
SSD_STATE = 64
SSD_CONV = 4
SSD_CHUNK = 128
SSD_XBC = SSD_INNER + 2 * SSD_GROUPS * SSD_STATE
CONF_WIDTH = 512
CONF_KERNEL = 31
SC_WIDTH = 512
SC_KERNEL = 3
N_BRANCH = 3
D_FF = 2816
FFN_KERNEL = 3
EPS = 1e-6

OFF_Z = SSD_INNER
OFF_XBC = OFF_Z + SSD_XBC
OFF_DT = OFF_XBC + SSD_HEADS
OFF_CONF = OFF_DT + 2 * CONF_WIDTH
OFF_SC = OFF_CONF + 3 * SC_WIDTH
N_IN = OFF_SC + N_BRANCH * D_MODEL
IN_SPLITS = (OFF_Z, OFF_XBC, OFF_DT, OFF_CONF, OFF_SC)

kernel_name = "hybrid_ssd_conformer_shortconv_adaln"


def rms_norm(x, g):
    xf = x.astype(jnp.float32)
    y = xf * lax.rsqrt(jnp.mean(xf * xf, axis=-1, keepdims=True) + EPS)
    return (y * g).astype(x.dtype)


def layer_norm(x, g, b):
    xf = x.astype(jnp.float32)
    mu = jnp.mean(xf, axis=-1, keepdims=True)
    xc = xf - mu
    y = xc * lax.rsqrt(jnp.mean(xc * xc, axis=-1, keepdims=True) + EPS)
    return (y * g + b).astype(x.dtype)


def gated_group_rmsnorm(y, z, g):
    v = (y * jax.nn.silu(z)).astype(jnp.float32)
    v = v.reshape(*v.shape[:-1], SSD_GROUPS, -1)
    v = v * lax.rsqrt(jnp.mean(v * v, axis=-1, keepdims=True) + EPS)
    return (v.reshape(y.shape) * g).astype(z.dtype)


def causal_dwconv(x, w, b=None):
    k = w.shape[0]
    y = lax.conv_general_dilated(
        x, w[:, None, :].astype(x.dtype), window_strides=(1,), padding=[(k - 1, 0)],
        dimension_numbers=('NWC', 'WIO', 'NWC'), feature_group_count=x.shape[-1])
    return y if b is None else y + b


def adaln(c, w, b):
    mod = jax.nn.silu(c) @ w + b
    shift, scale, gate = jnp.split(mod[:, None, :], 3, axis=-1)
    return shift, scale, gate


def ssd_chunked(x, dt, a, b_mat, c_mat):
    bsz, l, h, p = x.shape
    g, n = b_mat.shape[-2:]
    r = h // g
    q = SSD_CHUNK
    nc = l // q
    xc = (x * dt[..., None]).reshape(bsz, nc, q, g, r, p)
    a_dt = (dt * a).astype(jnp.float32).reshape(bsz, nc, q, g, r)
    a_cum = jnp.cumsum(jnp.moveaxis(a_dt, 2, -1), axis=-1)
    bc = b_mat.reshape(bsz, nc, q, g, n)
    cc = c_mat.reshape(bsz, nc, q, g, n)
    causal = jnp.tril(jnp.ones((q, q), dtype=bool))
    seg = a_cum[..., :, None] - a_cum[..., None, :]
    decay = jnp.exp(jnp.where(causal, seg, -jnp.inf))
    cb = jnp.einsum('bclgn,bcsgn->bcgls', cc, bc)
    wts = cb[:, :, :, None] * decay
    y_diag = jnp.einsum('bcgrls,bcsgrp->bclgrp', wts, xc)
    decay_states = jnp.exp(a_cum[..., -1:] - a_cum)
    xd = xc * jnp.moveaxis(decay_states, -1, 2)[..., None]
    states = jnp.einsum('bcsgn,bcsgrp->bcgrpn', bc, xd)
    chunk_decay = jnp.exp(a_cum[..., -1])

    def step(hstate, inp):
        s, d = inp
        return hstate * d[..., None, None] + s, hstate

    init = jnp.zeros((bsz, g, r, p, n), dtype=states.dtype)
    _, prev = lax.scan(step, init, (jnp.moveaxis(states, 1, 0), jnp.moveaxis(chunk_decay, 1, 0)))
    prev = jnp.moveaxis(prev, 0, 1)
    decay_out = jnp.exp(jnp.moveaxis(a_cum, -1, 2))[..., None]
    y_off = jnp.einsum('bclgn,bcgrpn->bclgrp', cc, prev) * decay_out
    return (y_diag + y_off).reshape(bsz, l, h, p)


def token_mixers(h, w_in, b_gate, ssd_conv_w, ssd_conv_b, ssd_dt_bias, ssd_a_log, ssd_d,
                 ssd_norm_g, w_ssd_out, conf_conv_w, conf_conv_b, conf_ln_g, conf_ln_b,
                 w_conf_out, sc_conv_w, w_sc_out, w_o):
    bsz, l, _ = h.shape
    proj = h @ w_in
    z, xbc, dt, conf_in, sc_in, gates = jnp.split(proj, IN_SPLITS, axis=-1)
    xbc = jax.nn.silu(causal_dwconv(xbc, ssd_conv_w, ssd_conv_b))
    xs, bm, cm = jnp.split(xbc, [SSD_INNER, SSD_INNER + SSD_GROUPS * SSD_STATE], axis=-1)
    dt = jax.nn.softplus((dt + ssd_dt_bias).astype(jnp.float32))
    a = -jnp.exp(ssd_a_log.astype(jnp.float32))
    xs = xs.reshape(bsz, l, SSD_HEADS, SSD_HEAD_DIM)
    y = ssd_chunked(xs, dt, a,
                    bm.reshape(bsz, l, SSD_GROUPS, SSD_STATE),
                    cm.reshape(bsz, l, SSD_GROUPS, SSD_STATE))
    y = (y + xs * ssd_d[:, None]).reshape(bsz, l, SSD_INNER)
    y_a = gated_group_rmsnorm(y, z, ssd_norm_g) @ w_ssd_out
    u_val, u_gate = jnp.split(conf_in, 2, axis=-1)
    u = u_val * jax.nn.sigmoid(u_gate)
    u = causal_dwconv(u, conf_conv_w, conf_conv_b)
    u = layer_norm(u, conf_ln_g, conf_ln_b)
    y_b = jax.nn.silu(u) @ w_conf_out
    gb, gc, xv = jnp.split(sc_in, 3, axis=-1)
    y_c = (gb * causal_dwconv(gc * xv, sc_conv_w)) @ w_sc_out
    g_a, g_b, g_c = jnp.split(jax.nn.sigmoid(gates + b_gate), 3, axis=-1)
    merged = g_a * y_a + g_b * y_b + g_c * y_c
    return (merged @ w_o).astype(h.dtype)


def conv_ffn(h, w_up, conv_w, conv_b, w_down):
    u = causal_dwconv(h @ w_up, conv_w, conv_b)
    gate, val = jnp.split(u, 2, axis=-1)
    return (jax.nn.silu(gate) * val) @ w_down


def setup_inputs(seed: int = 0) -> dict:
    key = jax.random.key(seed)
    ks = iter(jax.random.split(key, 40))

    def nrm(shape, scale):
        return jax.random.normal(next(ks), shape, jnp.float32) * scale

    def gain(shape):
        return 1.0 + nrm(shape, 0.02)

    dt0 = jnp.exp(jax.random.uniform(next(ks), (DEPTH, SSD_HEADS), jnp.float32,
                                     np.float32(np.log(1e-3)), np.float32(np.log(1e-1))))
    return {
        "x": nrm((BATCH, SEQ, D_MODEL), 1.0),
        "c": nrm((BATCH, D_MODEL), 1.0),
        "ada_mix_w": nrm((DEPTH, D_MODEL, 3 * D_MODEL), D_MODEL ** -0.5),
        "ada_mix_b": nrm((DEPTH, 3 * D_MODEL), 0.02),
        "norm_mix_g": gain((DEPTH, D_MODEL)),
        "w_in": nrm((DEPTH, D_MODEL, N_IN), D_MODEL ** -0.5),
        "b_gate": nrm((DEPTH, N_BRANCH * D_MODEL), 0.02),
        "ssd_conv_w": nrm((DEPTH, SSD_CONV, SSD_XBC), SSD_CONV ** -0.5),
        "ssd_conv_b": nrm((DEPTH, SSD_XBC), 0.02),
        "ssd_dt_bias": dt0 + jnp.log(-jnp.expm1(-dt0)),
        "ssd_a_log": jnp.log(jax.random.uniform(next(ks), (DEPTH, SSD_HEADS), jnp.float32, 1.0, 16.0)),
        "ssd_d": gain((DEPTH, SSD_HEADS)),
        "ssd_norm_g": gain((DEPTH, SSD_INNER)),
        "w_ssd_out": nrm((DEPTH, SSD_INNER, D_MODEL), SSD_INNER ** -0.5),
        "conf_conv_w": nrm((DEPTH, CONF_KERNEL, CONF_WIDTH), CONF_KERNEL ** -0.5),
        "conf_conv_b": nrm((DEPTH, CONF_WIDTH), 0.02),
        "conf_ln_g": gain((DEPTH, CONF_WIDTH)),
        "conf_ln_b": nrm((DEPTH, CONF_WIDTH), 0.02),
        "w_conf_out": nrm((DEPTH, CONF_WIDTH, D_MODEL), CONF_WIDTH ** -0.5),
        "sc_conv_w": nrm((DEPTH, SC_KERNEL, SC_WIDTH), SC_KERNEL ** -0.5),
        "w_sc_out": nrm((DEPTH, SC_WIDTH, D_MODEL), SC_WIDTH ** -0.5),
        "w_o": nrm((DEPTH, D_MODEL, D_MODEL), D_MODEL ** -0.5),
        "ada_ffn_w": nrm((DEPTH, D_MODEL, 3 * D_MODEL), D_MODEL ** -0.5),
        "ada_ffn_b": nrm((DEPTH, 3 * D_MODEL), 0.02),
        "norm_ffn_g": gain((DEPTH, D_MODEL)),
        "w_up": nrm((DEPTH, D_MODEL, 2 * D_FF), D_MODEL ** -0.5),
        "ffn_conv_w": nrm((DEPTH, FFN_KERNEL, 2 * D_FF), FFN_KERNEL ** -0.5),
        "ffn_conv_b": nrm((DEPTH, 2 * D_FF), 0.02),
        "w_down": nrm((DEPTH, D_FF, D_MODEL), D_FF ** -0.5),
        "final_norm_g": gain((D_MODEL,)),
    }


def reference(x, c, ada_mix_w, ada_mix_b, norm_mix_g, w_in, b_gate, ssd_conv_w, ssd_conv_b,
              ssd_dt_bias, ssd_a_log, ssd_d, ssd_norm_g, w_ssd_out, conf_conv_w, conf_conv_b,
              conf_ln_g, conf_ln_b, w_conf_out, sc_conv_w, w_sc_out, w_o, ada_ffn_w, ada_ffn_b,
              norm_ffn_g, w_up, ffn_conv_w, ffn_conv_b, w_down, final_norm_g):
    for i in range(DEPTH):
        shift, scale, gate = adaln(c, ada_mix_w[i], ada_mix_b[i])
        h = rms_norm(x, norm_mix_g[i]) * (1 + scale) + shift
        mix = token_mixers(h, w_in[i], b_gate[i], ssd_conv_w[i], ssd_conv_b[i], ssd_dt_bias[i],
                           ssd_a_log[i], ssd_d[i], ssd_norm_g[i], w_ssd_out[i], conf_conv_w[i],
                           conf_conv_b[i], conf_ln_g[i], conf_ln_b[i], w_conf_out[i],
                           sc_conv_w[i], w_sc_out[i], w_o[i])
        x = x + (gate * mix).astype(x.dtype)
        shift, scale, gate = adaln(c, ada_ffn_w[i], ada_ffn_b[i])
        h = rms_norm(x, norm_ffn_g[i]) * (1 + scale) + shift
        x = x + (gate * conv_ffn(h, w_up[i], ffn_conv_w[i], ffn_conv_b[i], w_down[i])).astype(x.dtype)
    return rms_norm(x, final_norm_g)
```

```python
import os
import numpy as np
from contextlib import ExitStack
import concourse.bass as bass
import concourse.mybir as mybir
from concourse.bass_utils import run_bass_kernel_spmd

F32 = mybir.dt.float32
BF16 = mybir.dt.bfloat16
AF = mybir.ActivationFunctionType
ALU = mybir.AluOpType
AX = mybir.AxisListType

D = 1024
DEPTH = 2
NIN = 7952
DFF = 2816
EPS = 1e-6
T = 512
OFF_XBC = 1024
OFF_DT = 2304
OFF_CONF = 2320
OFF_SC = 3344
OFF_G = 4880

VP = {}
_o = 0
for _n, _w in [("g_mix", 8), ("g_ffn", 8), ("ab_mix", 24), ("ab_ffn", 24), ("xw", 40), ("xb", 10),
               ("cw", 124), ("cb", 4), ("lng", 4), ("lnb", 4), ("sw", 12), ("bg", 24), ("ng", 8),
               ("fw", 132), ("fb", 44), ("gfin", 8)]:
    VP[_n] = _o
    _o += _w
NV = _o
NB = 16 + 16 + 16


def bc(ap, pos, n):
    l = [list(x) for x in ap.ap]
    l.insert(pos, [0, n])
    return bass.AP(ap.tensor, ap.offset, l)


def bclast(ap, n):
    l = [list(x) for x in ap.ap]
    assert l[-1][1] == 1
    l[-1] = [0, n]
    return bass.AP(ap.tensor, ap.offset, l)


class Sched:
    ENG = ("pe", "act", "dve", "pool", "sp")

    def __init__(self, nc, es):
        self.nc = nc
        self.es = es
        self.ops = {n: [] for n in self.ENG}
        self.count = {n: 0 for n in self.ENG}
        self.seen = {n: {} for n in self.ENG}
        self.sems = {n: es.enter_context(nc.semaphore("s_" + n)) for n in self.ENG}
        self.res = {}
        self.chans = []
        self.nbank = 0

    def chan(self, name):
        sem = self.es.enter_context(self.nc.semaphore("c_" + name))
        self.chans.append({"sem": sem, "count": 0})
        return len(self.chans) - 1

    def _r(self, k):
        r = self.res.get(k)
        if r is None:
            r = {"w": None, "rd": {}}
            self.res[k] = r
        return r

    def stage(self, name):
        self.nstage = getattr(self, "nstage", 0) + 1
        lim = int(os.environ.get("KSTAGE", "0"))
        if lim and self.nstage > lim and not getattr(self, "muted", False):
            self.muted = True
            print("MUTED before stage", self.nstage, name)

    def op(self, eng, fn, reads=(), writes=(), dma=None, force=False):
        if getattr(self, "muted", False) and not force:
            return
        waits = {}
        seen = self.seen[eng]

        def need(ev):
            if ev is None:
                return
            kind, who, val = ev
            if kind == "e" and who == eng and eng == "pe":
                return
            key = (kind, who)
            if seen.get(key, 0) >= val:
                return
            if waits.get(key, 0) < val:
                waits[key] = val

        for k in reads:
            r = self._r(k)
            need(r["w"])
            if isinstance(k, tuple) and k[0] == "ps":
                for ev in r["rd"].values():
                    if ev[1] != eng:
                        need(ev)
        for k in writes:
            r = self._r(k)
            need(r["w"])
            for ev in r["rd"].values():
                need(ev)
        for key, val in waits.items():
            seen[key] = val
        if dma is None:
            self.count[eng] += 1
            ev = ("e", eng, self.count[eng])
        else:
            c = self.chans[dma]
            c["count"] += 16
            ev = ("d", dma, c["count"])
        for k in reads:
            self._r(k)["rd"][(ev[0], ev[1])] = ev
        for k in writes:
            r = self._r(k)
            r["w"] = ev
            r["rd"] = {}
        self.ops[eng].append((waits, fn, ev))

    def final_waits(self, eng):
        waits = {}
        for i, c in enumerate(self.chans):
            if c["count"] > 0:
                waits[("d", i)] = c["count"]
        self.ops[eng].append((waits, None, None))

    def emit(self, name, e):
        for waits, fn, ev in self.ops[name]:
            for (kind, who), val in waits.items():
                sem = self.sems[who] if kind == "e" else self.chans[who]["sem"]
                e.wait_ge(sem, val)
            if fn is None:
                continue
            ins = fn(e)
            if ev[0] == "e":
                ins.then_inc(self.sems[name], 1)
            else:
                ins.then_inc(self.chans[ev[1]]["sem"], 16)


def build_nc(L, depth=DEPTH):
    NT = L // T
    nc = bass.Bass("TRN2", target_bir_lowering=False)
    dt_ = nc.dram_tensor
    xT = dt_("xT", [D, L], F32, kind="ExternalInput").ap()
    cpk = dt_("cpk", [128, 8], F32, kind="ExternalInput").ap()
    vp_d = dt_("vp", [depth, 128, NV], F32, kind="ExternalInput").ap()
    bp_d = dt_("bp", [depth, 128, NB], F32, kind="ExternalInput").ap()
    wsrc = {}
    wshape = {"ada_mix_w": [D, 3 * D], "ada_ffn_w": [D, 3 * D], "w_in": [D, NIN], "w_ssd_out": [D, D],
              "w_conf_out": [512, D], "w_sc_out": [512, D], "w_o": [D, D], "w_up": [D, 2 * DFF],
              "w_down": [DFF, D]}
    wbf = {}
    for n, shp in wshape.items():
        wsrc[n] = dt_(n, [depth] + shp, F32, kind="ExternalInput").ap()
        wbf[n] = dt_(n + "_bf", [depth] + shp, BF16, kind="Internal").ap()
    outT = dt_("outT", [D, L], F32, kind="ExternalOutput").ap()
    dgw = dt_("dgw", [depth, 4, 128, 31 * 128], BF16, kind="Internal").ap()
    dgx = dt_("dgx", [depth, 2, 128, 20 * 128], BF16, kind="Internal").ap()
    dgs = dt_("dgs", [depth, 128, 12 * 128], BF16, kind="Internal").ap()

    es = ExitStack()
    with es:
        S = Sched(nc, es)

        def sb(name, shape, dtype):
            return es.enter_context(nc.sbuf_tensor(name, shape, dtype))

        x_sb = sb("x_sb", [128, 8, T], F32)
        bufA = sb("bufA", [128, 8, T], BF16)
        bufB = sb("bufB", [128, 8, T], BF16)
        rstd = sb("rstd", [128, T], F32)
        NWK = 5
        wk = [sb(f"wk{i}", [128, T], F32) for i in range(NWK)]
        NW = 5
        wslot = [sb(f"ws{i}", [128, 4096], BF16) for i in range(NW)]
        wchan = [S.chan(f"w{i}") for i in range(NW)]
        Sst = [sb(f"S{l}", [128, 512], F32) for l in range(depth)]
        Sbf = [sb(f"Sb{l}", [128, 512], BF16) for l in range(depth)]
        xbc_raw = sb("xbc_raw", [128, 10, 3 + T], BF16)
        xbcs = sb("xbcs", [128, 10, T], BF16)
        u_raw = sb("u_raw", [128, 4, 30 + T], BF16)
        cacc = sb("cacc", [128, 4, T], F32)
        sbT = sb("sbT", [128, 4, T], BF16)
        p_raw = sb("p_raw", [128, 4, 2 + T], BF16)
        gb_sb = sb("gb_sb", [128, 4, T], BF16)
        q_bf = sb("q_bf", [128, 4, T], BF16)
        sig = sb("sig", [128, 6, T], BF16)
        a_bf = sb("a_bf", [128, 11, T], BF16)
        a_bf2 = sb("a_bf2", [128, 11, T], BF16)
        halo_x = [sb(f"hx{l}", [128, 10, 3], BF16) for l in range(depth)]
        halo_u = [sb(f"hu{l}", [128, 4, 30], BF16) for l in range(depth)]
        halo_p = [sb(f"hp{l}", [128, 4, 2], BF16) for l in range(depth)]
        tails = [sb(f"tl{l}", [128, 44, 2], F32) for l in range(depth)]
        zsb = [sb(f"zs{i}", [128, 1024], BF16) for i in range(2)]
        decT = [sb(f"decT{i}", [128, 4, 128], BF16) for i in range(2)]
        wts = sb("wts", [128, 16, 128], BF16)
        cbt = sb("cbt", [128, 2, 128], BF16)
        xs_tokb = [sb(f"xs_tok{i}", [128, 1024], BF16) for i in range(2)]
        xc = sb("xc", [128, 1024], BF16)
        xd = sb("xd", [128, 1024], BF16)
        b_tok = sb("b_tok", [128, 128], BF16)
        t1 = sb("t1", [128, 1024], F32)
        t2 = sb("t2", [128, 1024], F32)
        vn = sb("vn", [128, 1024], BF16)
        dtx = sb("dtx", [128, 64], F32)
        dta = sb("dta", [128, 64], F32)
        dte = sb("dte", [128, 64], F32)
        dtp = sb("dtp", [128, 64], F32)
        adt = sb("adt", [128, 64], F32)
        ahl = sb("ahl", [128, 128], BF16)
        nacum = sb("nacum", [128, 64], F32)
        eacum = sb("eacum", [128, 64], F32)
        acl = sb("acl", [128, 16], F32)
        dsx = sb("dsx", [128, 16], F32)
        ds = sb("ds", [128, 16], F32)
        cd = sb("cd", [128, 16], F32)
        ssq = sb("ssq", [128, 2], F32)
        rsv = sb("rsv", [128, 2], F32)
        ident = sb("ident", [128, 128], BF16)
        tri = sb("tri", [128, 128], BF16)
        mneg = sb("mneg", [128, 128], BF16)
        ones_bf = sb("ones_bf", [128, 128], BF16)
        ones_f = sb("ones_f", [128, 128], F32)
        wconst = sb("wconst", [128, 512], BF16)
        vp = [sb(f"vp{l}", [128, NV], F32) for l in range(depth)]
        bp = [sb(f"bp{l}", [128, NB], F32) for l in range(depth)]
        arep = [sb(f"arep{l}", [128, 16], F32) for l in range(depth)]
        mod = {(l, w): sb(f"mod{l}{w}", [128, 24], F32) for l in range(depth) for w in ("mix", "ffn")}
        gsv = {(l, w): sb(f"gs{l}{w}", [128, 8], F32) for l in range(depth) for w in ("mix", "ffn")}
        c_sb = sb("c_sb", [128, 8], F32)
        c_sig = sb("c_sig", [128, 8], F32)
        c_bf = sb("c_bf", [128, 8], BF16)
        banks = [es.enter_context(nc.psum_tensor(f"pb{i}", [128, 512], F32)) for i in range(8)]

        ch_misc = S.chan("misc")
        ch_x = S.chan("x")
        ch_out = S.chan("out")

        def bank():
            i = S.nbank % 7
            S.nbank += 1
            return banks[i], ("ps", i)

        def warm(n):
            for _ in range(n):
                mm(banks[7][:], ones_bf[:], wconst[:], True, True, ["const"], [("ps", 7)])

        nwk = [0]

        def work():
            i = nwk[0] % NWK
            nwk[0] += 1
            return wk[i], ("wk", i)

        nws = [0]

        def wload(pieces, key_reads):
            i = nws[0] % NW
            nws[0] += 1
            slot = wslot[i]
            for (src, nkc, width, coff, ncols) in pieces:
                dst = slot[:, 0:nkc * width].rearrange("p (k c) -> p k c", k=nkc)[:, :, coff:coff + ncols]
                s3 = src.rearrange("(k p) c -> p k c", p=128)
                S.op("sp", (lambda d, s: (lambda e: e.dma_start(out=d, in_=s)))(dst, s3),
                     reads=key_reads, writes=[("ws", i)], dma=wchan[i])
            return slot, ("ws", i)

        def wview(slot, nkc, width):
            return slot[:, 0:nkc * width].rearrange("p (k c) -> p k c", k=nkc)

        def mm(out, lhsT, rhs, start, stop, reads, writes):
            S.op("pe", lambda e: e.matmul(out, lhsT, rhs, start=start, stop=stop), reads, writes)

        def tr(out, in_, reads, writes):
            S.op("pe", lambda e: e.transpose(out, in_, ident[:]), reads + ["const"], writes)

        def act(out, in_, func, reads, writes, bias=None, scale=None):
            kw = {}
            if bias is not None:
                kw["bias"] = bias
            if scale is not None:
                kw["scale"] = scale
            S.op("act", lambda e: e.activation(out=out, in_=in_, func=func, **kw), reads, writes)

        def tt(eng, out, in0, in1, op, reads, writes):
            S.op(eng, lambda e: e.tensor_tensor(out=out, in0=in0, in1=in1, op=op), reads, writes)

        def ts(eng, out, in0, s1, s2, op0, op1, reads, writes):
            if op1 is None:
                S.op(eng, lambda e: e.tensor_scalar(out=out, in0=in0, scalar1=s1, scalar2=None, op0=op0), reads, writes)
            else:
                S.op(eng, lambda e: e.tensor_scalar(out=out, in0=in0, scalar1=s1, scalar2=s2, op0=op0, op1=op1), reads, writes)

        def stt(eng, out, in0, scalar, in1, op0, op1, reads, writes):
            S.op(eng, lambda e: e.scalar_tensor_tensor(out=out, in0=in0, scalar=scalar, in1=in1, op0=op0, op1=op1),
                 reads, writes)

        def cp(eng, out, in_, reads, writes):
            if eng == "act":
                S.op(eng, lambda e: e.activation(out=out, in_=in_, func=AF.Copy), reads, writes)
            else:
                S.op(eng, lambda e: e.tensor_copy(out=out, in_=in_), reads, writes)

        def rsqrt_inplace(ap, key, scale):
            ts("dve", ap, ap, scale, EPS, ALU.mult, ALU.add, [key], [key])
            act(ap, ap, AF.Ln, [key], [key])
            act(ap, ap, AF.Exp, [key], [key], scale=-0.5)

        cast_order = []
        for l in range(depth):
            cast_order += [("ada_mix_w", l), ("ada_ffn_w", l)]
        for l in range(depth):
            cast_order += [("w_in", l), ("w_ssd_out", l), ("w_conf_out", l), ("w_sc_out", l), ("w_o", l),
                           ("w_up", l), ("w_down", l)]
        for (n, l) in cast_order:
            rows = wshape[n][0]
            step = 256
            ch_cast = S.chan("cast_%s%d" % (n, l))
            for r0 in range(0, rows, step):
                r1 = min(rows, r0 + step)
                S.op("pool", (lambda d, s: (lambda e: e.dma_start(out=d, in_=s)))(wbf[n][l, r0:r1, :], wsrc[n][l, r0:r1, :]),
                     reads=[], writes=[("wb", n, l)], dma=ch_cast)

        S.op("pool", lambda e: e.memset(ident[:], 1.0), [], ["const"])
        S.op("pool", lambda e: e.affine_select(out=ident[:], in_=ident[:], pattern=[[1, 128]], compare_op=ALU.is_equal,
                                               fill=0.0, base=0, channel_multiplier=-1), [], ["const"])
        S.op("pool", lambda e: e.memset(tri[:], 1.0), [], ["const"])
        S.op("pool", lambda e: e.affine_select(out=tri[:], in_=tri[:], pattern=[[1, 128]], compare_op=ALU.is_ge,
                                               fill=0.0, base=0, channel_multiplier=-1), [], ["const"])
        S.op("pool", lambda e: e.memset(mneg[:], 0.0), [], ["const"])
        S.op("pool", lambda e: e.affine_select(out=mneg[:], in_=mneg[:], pattern=[[1, 128]], compare_op=ALU.is_ge,
                                               fill=-30000.0, base=0, channel_multiplier=-1), [], ["const"])
        S.op("pool", lambda e: e.memset(ones_bf[:], 1.0), [], ["const"])
        S.op("pool", lambda e: e.memset(ones_f[:], 1.0), [], ["const"])
        S.op("pool", lambda e: e.memset(wconst[:], 0.5), [], ["const"])
        for l in range(depth):
            S.op("pool", (lambda a: (lambda e: e.memset(a, 0.0)))(Sst[l][:]), [], [("S", l)])
            S.op("pool", (lambda a: (lambda e: e.memset(a, 0.0)))(Sbf[l][:]), [], [("Sb", l)])
            S.op("pool", (lambda a: (lambda e: e.memset(a, 0.0)))(halo_x[l][:]), [], [("hx", l)])
            S.op("pool", (lambda a: (lambda e: e.memset(a, 0.0)))(halo_u[l][:]), [], [("hu", l)])
            S.op("pool", (lambda a: (lambda e: e.memset(a, 0.0)))(halo_p[l][:]), [], [("hp", l)])
            S.op("pool", (lambda a: (lambda e: e.memset(a, 0.0)))(tails[l][:]), [], [("tl", l)])
            S.op("sp", (lambda d, s: (lambda e: e.dma_start(out=d, in_=s)))(vp[l][:], vp_d[l]), [], [("vp", l)], dma=S.chan("vp%d" % l))
            S.op("sp", (lambda d, s: (lambda e: e.dma_start(out=d, in_=s)))(bp[l][:], bp_d[l]), [], [("bp", l)], dma=S.chan("bp%d" % l))
        S.op("sp", lambda e: e.dma_start(out=c_sb[:], in_=cpk), [], ["c"], dma=ch_misc)
        act(c_sig[:], c_sb[:], AF.Sigmoid, ["c"], ["csig"])
        tt("dve", c_bf[:], c_sb[:], c_sig[:], ALU.mult, ["c", "csig"], ["cbf"])
        for l in range(depth):
            act(arep[l][:], bp[l][:, 16:32], AF.Exp, [("bp", l)], [("arep", l)])
            ts("dve", arep[l][:], arep[l][:], -1.0, None, ALU.mult, None, [("arep", l)], [("arep", l)])
            for w in ("mix", "ffn"):
                wn = "ada_%s_w" % w
                bk, bkey = bank()
                for blk in range(6):
                    slot, wkey = wload([(wbf[wn][l, :, blk * 512:(blk + 1) * 512], 8, 512, 0, 512)], [("wb", wn, l)])
                    wv = wview(slot, 8, 512)
                    for oc4 in range(4):
                        oc = blk * 4 + oc4
                        for kc in range(8):
                            mm(bk[:, oc:oc + 1], wv[:, kc, oc4 * 128:(oc4 + 1) * 128], c_bf[:, kc:kc + 1],
                               kc == 0, kc == 7, [wkey, "cbf"], [bkey])
                ab = VP["ab_" + w]
                tt("dve", mod[(l, w)][:], bk[:, 0:24], vp[l][:, ab:ab + 24], ALU.add, [bkey, ("vp", l)], [("mod", l, w)])
                g0 = VP["g_" + w]
                stt("dve", gsv[(l, w)][:], mod[(l, w)][:, 8:16], 1.0, vp[l][:, g0:g0 + 8], ALU.add, ALU.mult,
                    [("mod", l, w), ("vp", l)], [("gs", l, w)])

        for l in range(depth):
            for i in range(4):
                si = nws[0] % NW
                nws[0] += 1
                cw = VP["cw"] + i * 31
                for k in range(31):
                    ts("dve", wslot[si][:, k * 128:(k + 1) * 128], ident[:], vp[l][:, cw + k:cw + k + 1], None, ALU.mult, None,
                       ["const", ("vp", l)], [("ws", si)])
                S.op("sp", (lambda d, s_: (lambda e: e.dma_start(out=d, in_=s_)))(dgw[l, i], wslot[si][:, 0:31 * 128]),
                     [("ws", si)], [("dgw", l, i)], dma=S.chan("dgw%d_%d" % (l, i)))
        for l in range(depth):
            for gi in range(2):
                si = nws[0] % NW
                nws[0] += 1
                for cj in range(5):
                    xw = VP["xw"] + (gi * 5 + cj) * 4
                    for k in range(4):
                        c_ = (cj * 4 + k) * 128
                        ts("dve", wslot[si][:, c_:c_ + 128], ident[:], vp[l][:, xw + k:xw + k + 1], None, ALU.mult, None,
                           ["const", ("vp", l)], [("ws", si)])
                S.op("sp", (lambda d, s_: (lambda e: e.dma_start(out=d, in_=s_)))(dgx[l, gi], wslot[si][:, 0:20 * 128]),
                     [("ws", si)], [("dgx", l, gi)], dma=S.chan("dgx%d_%d" % (l, gi)))
            si = nws[0] % NW
            nws[0] += 1
            for i in range(4):
                sw = VP["sw"] + i * 3
                for k in range(3):
                    c_ = (i * 3 + k) * 128
                    ts("dve", wslot[si][:, c_:c_ + 128], ident[:], vp[l][:, sw + k:sw + k + 1], None, ALU.mult, None,
                       ["const", ("vp", l)], [("ws", si)])
            S.op("sp", (lambda d, s_: (lambda e: e.dma_start(out=d, in_=s_)))(dgs[l], wslot[si][:, 0:12 * 128]),
                 [("ws", si)], [("dgs", l)], dma=S.chan("dgs%d" % l))
        xkeys = [("x", j) for j in range(8)]
        Akeys = [("bufA", j) for j in range(8)]
        Bkeys = [("bufB", j) for j in range(8)]

        def rms(l):
            bk, bkey = bank()
            for kc in range(8):
                act(bufA[:, kc, :], x_sb[:, kc, :], AF.Square, [("x", kc)], [("bufA", kc)])
                mm(bk[:], ones_bf[:], bufA[:, kc, :], kc == 0, kc == 7, [("bufA", kc), "const"], [bkey])
            cp("dve", rstd[:], bk[:], [bkey], ["rstd"])
            rsqrt_inplace(rstd[:], "rstd", 1.0 / D)

        def make_h(l, w):
            rms(l)
            warm(48)
            sh = mod[(l, w)]
            for kc in range(8):
                wt_, wkk = work()
                stt("dve", wt_[:], x_sb[:, kc, :], gsv[(l, w)][:, kc:kc + 1], rstd[:], ALU.mult, ALU.mult,
                    [("x", kc), ("gs", l, w), "rstd"], [wkk])
                act(bufB[:, kc, :], wt_[:], AF.Identity, [wkk, ("mod", l, w)], [("bufB", kc)], bias=sh[:, kc:kc + 1])

        def proj_cm(wv, wkey, c0, nk, rhs_fn, rhs_keys):
            bk, bkey = bank()
            for kc in range(nk):
                mm(bk[:], wv[:, kc, c0:c0 + 128], rhs_fn(kc), kc == 0, kc == nk - 1, [wkey] + rhs_keys(kc), [bkey])
            return bk, bkey

        hB = (lambda kc: bufB[:, kc, :])
        hBk = (lambda kc: [("bufB", kc)])

        def mixer(l):
            V = vp[l]
            wi = wbf["w_in"][l]
            wbk = [("wb", "w_in", l)]
            S.stage("mix_h")
            make_h(l, "mix")
            S.stage("xbc")
            wd, wdk = wload([(wi[:, OFF_DT:OFF_DT + 16], 8, 16, 0, 16)], wbk)
            wdv = wview(wd, 8, 16)
            BP = bp[l]
            bkd, bkdk = bank()
            for q in range(4):
                qs = slice(q * 128, (q + 1) * 128)
                for kc in range(8):
                    mm(bkd[:, q * 16:(q + 1) * 16], bufB[:, kc, qs], wdv[:, kc, :], kc == 0, kc == 7, [wdk, ("bufB", kc)], [bkdk])
            q4 = lambda ap: ap.rearrange("p (q h) -> p q h", q=4)
            tt("dve", q4(dtx[:]), q4(bkd[:, 0:64]), bc(BP[:, 0:16], 1, 4), ALU.add, [bkdk, ("bp", l)], ["dtx"])
            act(dta[:], dtx[:], AF.Abs, ["dtx"], ["dta"])
            act(dte[:], dta[:], AF.Exp, ["dta"], ["dte"], scale=-1.0)
            act(dte[:], dte[:], AF.Ln, ["dte"], ["dte"], bias=1.0)
            stt("dve", dtp[:], dtx[:], 0.0, dte[:], ALU.max, ALU.add, ["dtx", "dte"], ["dtp"])
            tt("dve", q4(adt[:]), q4(dtp[:]), bc(arep[l][:], 1, 4), ALU.mult, ["dtp", ("arep", l)], ["adt"])
            cp("dve", ahl[:, 0:64], adt[:], ["adt"], ["ahl"])
            tt("dve", ahl[:, 64:128], adt[:], ahl[:, 0:64], ALU.subtract, ["adt", "ahl"], ["ahl"])
            cp("pool", xbc_raw[:, :, 0:3], halo_x[l][:], [("hx", l)], ["xraw_h"])
            for (b0, ncol) in ((0, 512), (512, 512), (1024, 256)):
                slot, wkey = wload([(wi[:, OFF_XBC + b0:OFF_XBC + b0 + ncol], 8, 512, 0, ncol)], wbk)
                wv = wview(slot, 8, 512)
                for i in range(ncol // 128):
                    ci = b0 // 128 + i
                    bk, bkey = proj_cm(wv, wkey, i * 128, 8, hB, hBk)
                    cp("act", xbc_raw[:, ci, 3:3 + T], bk[:], [bkey], [("xraw", ci)])
            S.stage("conf1")
            cp("pool", u_raw[:, :, 0:30], halo_u[l][:], [("hu", l)], ["uraw_h"])
            sv, svk = wload([(wi[:, OFF_CONF:OFF_CONF + 512], 8, 512, 0, 512)], wbk)
            sg_, sgk = wload([(wi[:, OFF_CONF + 512:OFF_CONF + 1024], 8, 512, 0, 512)], wbk)
            svv, sgv = wview(sv, 8, 512), wview(sg_, 8, 512)
            for i in range(4):
                bv, bvk = proj_cm(svv, svk, i * 128, 8, hB, hBk)
                bg, bgk = proj_cm(sgv, sgk, i * 128, 8, hB, hBk)
                sgm, sgmk = work()
                act(sgm[:], bg[:], AF.Sigmoid, [bgk], [sgmk])
                tt("dve", u_raw[:, i, 30:30 + T], bv[:], sgm[:], ALU.mult, [bvk, sgmk], [("uraw", i)])
            S.stage("sc1")
            cp("pool", p_raw[:, :, 0:2], halo_p[l][:], [("hp", l)], ["praw_h"])
            wsl = [wload([(wi[:, OFF_SC + j * 512:OFF_SC + (j + 1) * 512], 8, 512, 0, 512)], wbk) for j in range(3)]
            wvv = [wview(s_, 8, 512) for (s_, _) in wsl]
            for i in range(4):
                bgb, bgbk = proj_cm(wvv[0], wsl[0][1], i * 128, 8, hB, hBk)
                cp("act", gb_sb[:, i, :], bgb[:], [bgbk], [("gb", i)])
                bgc, bgck = proj_cm(wvv[1], wsl[1][1], i * 128, 8, hB, hBk)
                gcs, gck = work()
                cp("act", gcs[:], bgc[:], [bgck], [gck])
                bxv, bxvk = proj_cm(wvv[2], wsl[2][1], i * 128, 8, hB, hBk)
                tt("dve", p_raw[:, i, 2:2 + T], bxv[:], gcs[:], ALU.mult, [bxvk, gck], [("praw", i)])
            for gi in range(2):
                xsl, xkey = wload([(dgx[l, gi], 1, 20 * 128, 0, 20 * 128)], [("dgx", l, gi)])
                for cj in range(5):
                    ci = gi * 5 + cj
                    bk, bkey = bank()
                    for k in range(4):
                        c_ = (cj * 4 + k) * 128
                        mm(bk[:], xsl[:, c_:c_ + 128], xbc_raw[:, ci, k:k + T], k == 0, k == 3,
                           [xkey, ("xraw", ci), "xraw_h"], [bkey])
                    act(xbcs[:, ci, :], bk[:], AF.Silu, [bkey, ("vp", l)], [("xbcs", ci)],
                        bias=V[:, VP["xb"] + ci:VP["xb"] + ci + 1])
            cp("pool", halo_x[l][:], xbc_raw[:, :, T:T + 3], [("xraw", ci) for ci in range(10)], [("hx", l)])
            def conf_conv_pe(i):
                dsl, dkey = wload([(dgw[l, i], 1, 31 * 128, 0, 31 * 128)], [("dgw", l, i)])
                bk, bkey = bank()
                for k in range(31):
                    mm(bk[:], dsl[:, k * 128:(k + 1) * 128], u_raw[:, i, k:k + T], k == 0, k == 30,
                       [dkey, ("uraw", i), "uraw_h"], [bkey])
                act(cacc[:, i, :], bk[:], AF.Identity, [bkey, ("vp", l)], [("cacc", i)],
                    bias=V[:, VP["cb"] + i:VP["cb"] + i + 1])

            pend = []

            def fill(n=1):
                for _ in range(n):
                    if not pend:
                        return
                    k, i = pend.pop(0)
                    cw = VP["cw"] + i * 31
                    stt("dve", cacc[:, i, :], u_raw[:, i, k:k + T], V[:, cw + k:cw + k + 1], cacc[:, i, :], ALU.mult, ALU.add,
                        [("uraw", i), "uraw_h", ("vp", l), ("cacc", i)], [("cacc", i)])
            S.stage("ssd")
            z0, zk0 = wload([(wi[:, 0:512], 8, 512, 0, 512)], wbk)
            z1, zk1 = wload([(wi[:, 512:1024], 8, 512, 0, 512)], wbk)
            zv = [wview(z0, 8, 512), wview(z1, 8, 512)]
            zk = [zk0, zk1]
            bka, bkak = bank()
            for q in range(4):
                mm(bka[:, q * 16:(q + 1) * 16], tri[:], ahl[:, q * 16:(q + 1) * 16], True, False, ["const", "ahl"], [bkak])
                mm(bka[:, q * 16:(q + 1) * 16], tri[:], ahl[:, 64 + q * 16:64 + (q + 1) * 16], False, True, ["const", "ahl"], [bkak])
            act(nacum[:], bka[:, 0:64], AF.Copy, [bkak], ["nacum"], scale=-1.0)
            act(eacum[:], bka[:, 0:64], AF.Exp, [bkak], ["eacum"])
            def front(q):
                qs = slice(q * 128, (q + 1) * 128)
                zq = zsb[q % 2]
                xq = xs_tokb[q % 2]
                for zb in range(2):
                    bk, bkey = bank()
                    for kc in range(8):
                        mm(bk[:], bufB[:, kc, qs], zv[zb][:, kc, :], kc == 0, kc == 7, [zk[zb], ("bufB", kc)], [bkey])
                    act(zq[:, zb * 512:(zb + 1) * 512], bk[:], AF.Silu, [bkey], [("zs", q % 2, zb)])
                    yield
                conf_conv_pe(q)
                yield
                for g in range(2):
                    gp = slice(g * 64, (g + 1) * 64)
                    bk, bkey = bank()
                    mm(bk[:, 0:128], xbcs[gp, 8, qs], xbcs[gp, 9, qs], True, True, [("xbcs", 8), ("xbcs", 9)], [bkey])
                    cp("act", cbt[:, g, :], bk[:, 0:128], [bkey], ["cbt"])
                yield
                for hq in range(4):
                    bk, bkey = bank()
                    for i in range(4):
                        h = hq * 4 + i
                        o = bk[:, i * 128:(i + 1) * 128]
                        mm(o, bclast(ahl[:, q * 16 + h:q * 16 + h + 1], 128), tri[:], True, False, ["ahl", "const"], [bkey])
                        mm(o, bclast(ahl[:, 64 + q * 16 + h:64 + q * 16 + h + 1], 128), tri[:], False, False, ["ahl", "const"], [bkey])
                        mm(o, ident[:], mneg[:], False, True, ["const"], [bkey])
                    dT = decT[hq % 2]
                    dk = ("decT", hq % 2)
                    for i in range(4):
                        h = hq * 4 + i
                        act(dT[:, i, :], bk[:, i * 128:(i + 1) * 128], AF.Exp, [bkey, "nacum"], [dk],
                            bias=nacum[:, q * 16 + h:q * 16 + h + 1])
                    cp("dve", acl[:, hq * 4:(hq + 1) * 4], bk[:, 127::128], [bkey], ["acl"])
                    tt("dve", wts[:, hq * 4:(hq + 1) * 4, :], dT[:], bc(cbt[:, hq // 2, :], 1, 4), ALU.mult,
                       [dk, "cbt"], [("wts", hq)])
                    yield
                tt("dve", dsx[:], acl[:], nacum[:, q * 16:(q + 1) * 16], ALU.add, ["acl", "nacum"], ["dsx"])
                act(ds[:], dsx[:], AF.Exp, ["dsx"], ["ds"])
                act(cd[:], acl[:], AF.Exp, ["acl"], ["cd"])
                yield
                bk, bkey = bank()
                bkb = bk[:].bitcast(BF16)
                for c in range(8):
                    tr(bkb[:, c * 128:(c + 1) * 128], xbcs[:, c, qs], [("xbcs", c)], [bkey])
                cp("act", xq[:], bkb, [bkey], [("xs_tok", q % 2)])
                tt("dve", xc[:].rearrange("p (h d) -> p h d", h=16), bkb.rearrange("p (h d) -> p h d", h=16),
                   bc(dtp[:, q * 16:(q + 1) * 16], 2, 64), ALU.mult, [bkey, "dtp"], ["xc"])
                yield
                tt("pool", xd[:].rearrange("p (h d) -> p h d", h=16), xc[:].rearrange("p (h d) -> p h d", h=16),
                   bc(ds[:], 2, 64), ALU.mult, ["xc", "ds"], ["xd"])
                bk, bkey = bank()
                bkb2 = bk[:].bitcast(BF16)
                tr(bkb2[:, 0:128], xbcs[:, 8, qs], [("xbcs", 8)], [bkey])
                cp("act", b_tok[:], bkb2[:, 0:128], [bkey], ["b_tok"])
                yield

            def mid(q):
                qs = slice(q * 128, (q + 1) * 128)
                by = [bank(), bank()]
                for h in range(16):
                    b_, bk_ = by[h // 8]
                    mm(b_[:, (h % 8) * 64:(h % 8 + 1) * 64], wts[:, h, :], xc[:, h * 64:(h + 1) * 64], True, True,
                       [("wts", h // 4), "xc"], [bk_])
                bo = [bank(), bank()]
                for g in range(2):
                    gp = slice(g * 64, (g + 1) * 64)
                    mm(bo[g][0][:], xbcs[gp, 9, qs], Sbf[l][gp, :], True, True, [("xbcs", 9), ("Sb", l)], [bo[g][1]])
                for g in range(2):
                    hs = slice(g * 512, (g + 1) * 512)
                    tt("dve", t1[:, hs].rearrange("p (h d) -> p h d", h=8), bo[g][0][:].rearrange("p (h d) -> p h d", h=8),
                       bc(eacum[:, q * 16 + g * 8:q * 16 + (g + 1) * 8], 2, 64), ALU.mult, [bo[g][1], "eacum"], [("t1", g)])
                    tt("dve", t1[:, hs], t1[:, hs], by[g][0][:], ALU.add, [("t1", g), by[g][1]], [("t1", g)])
                yield
                bs = [bank(), bank()]
                for g in range(2):
                    mm(bs[g][0][:], b_tok[:], xd[:, g * 512:(g + 1) * 512], True, True, ["b_tok", "xd"], [bs[g][1]])
                for g in range(2):
                    gp = slice(g * 64, (g + 1) * 64)
                    tt("dve", Sst[l][gp, :].rearrange("p (h d) -> p h d", h=8), Sst[l][gp, :].rearrange("p (h d) -> p h d", h=8),
                       bc(cd[gp, g * 8:(g + 1) * 8], 2, 64), ALU.mult, [("S", l), "cd"], [("S", l)])
                    tt("dve", Sst[l][gp, :], Sst[l][gp, :], bs[g][0][gp, :], ALU.add, [("S", l), bs[g][1]], [("S", l)])
                cp("act", Sbf[l][:], Sst[l][:], [("S", l)], [("Sb", l)])
                yield

            def tail(q):
                qs = slice(q * 128, (q + 1) * 128)
                zq = zsb[q % 2]
                xq = xs_tokb[q % 2]
                zkeys = [("zs", q % 2, 0), ("zs", q % 2, 1)]
                tt("pool", t2[:].rearrange("p (h d) -> p h d", h=16), xq[:].rearrange("p (h d) -> p h d", h=16),
                   bc(BP[:, 32:48], 2, 64), ALU.mult, [("xs_tok", q % 2), ("bp", l)], ["t2"])
                yield
                tt("dve", t1[:], t1[:], t2[:], ALU.add, [("t1", 0), ("t1", 1), "t2"], [("t1", 0), ("t1", 1)])
                yield
                tt("dve", t1[:], t1[:], zq[:], ALU.mult, [("t1", 0), ("t1", 1)] + zkeys, [("t1", 0), ("t1", 1)])
                yield
                act(t2[:], t1[:], AF.Square, [("t1", 0), ("t1", 1)], ["t2"])
                yield
                S.op("dve", lambda e: e.reduce_sum(out=ssq[:], in_=t2[:].rearrange("p (g d) -> p g d", g=2), axis=AX.X),
                     ["t2"], ["ssq"])
                cp("dve", rsv[:], ssq[:], ["ssq"], ["rsv"])
                ts("dve", rsv[:], rsv[:], 1.0 / 512, EPS, ALU.mult, ALU.add, ["rsv"], ["rsv"])
                yield
                act(rsv[:], rsv[:], AF.Ln, ["rsv"], ["rsv"])
                act(rsv[:], rsv[:], AF.Exp, ["rsv"], ["rsv"], scale=-0.5)
                yield
                for g in range(2):
                    hs = slice(g * 512, (g + 1) * 512)
                    ts("dve", vn[:, hs], t1[:, hs], rsv[:, g:g + 1], None, ALU.mult, None, [("t1", g), "rsv"], ["vn"])
                yield
                bk, bkey = bank()
                bkb = bk[:].bitcast(BF16)
                for c in range(8):
                    tr(bkb[:, c * 128:(c + 1) * 128], vn[:, c * 128:(c + 1) * 128], ["vn"], [bkey])
                tt("dve", bufA[:, :, qs], bkb.rearrange("p (c t) -> p c t", c=8), bc(V[:, VP["ng"]:VP["ng"] + 8], 2, 128),
                   ALU.mult, [bkey, ("vp", l)], Akeys)
                yield

            def run(g):
                for _ in g:
                    pass

            def zipper(a, b):
                a_live, b_live = True, True
                while a_live or b_live:
                    if a_live:
                        try:
                            next(a)
                        except StopIteration:
                            a_live = False
                    if b_live:
                        try:
                            next(b)
                        except StopIteration:
                            b_live = False

            def front_mid(q):
                yield from front(q)
                yield from mid(q)

            run(front_mid(0))
            for q in range(1, 4):
                zipper(tail(q - 1), front_mid(q))
            run(tail(3))
            fill(1000)
            S.stage("conf2")
            cp("pool", halo_u[l][:], u_raw[:, :, T:T + 30], [("uraw", i) for i in range(4)], [("hu", l)])
            bm, bmk = bank()
            bq, bqk = bank()
            for i in range(4):
                sq_, sqk = work()
                act(sq_[:], cacc[:, i, :], AF.Square, [("cacc", i)], [sqk])
                mm(bm[:], ones_f[:], cacc[:, i, :], i == 0, i == 3, ["const", ("cacc", i)], [bmk])
                mm(bq[:], ones_f[:], sq_[:], i == 0, i == 3, ["const", sqk], [bqk])
            mean, mk_ = work()
            msq, msk = work()
            rc, rck = work()
            ts("dve", mean[:], bm[:], 1.0 / 512, None, ALU.mult, None, [bmk], [mk_])
            tt("dve", msq[:], mean[:], mean[:], ALU.mult, [mk_], [msk])
            stt("dve", rc[:], bq[:], 1.0 / 512, msq[:], ALU.mult, ALU.subtract, [bqk, msk], [rck])
            rsqrt_inplace(rc[:], rck, 1.0)
            for i in range(4):
                tt("dve", cacc[:, i, :], cacc[:, i, :], mean[:], ALU.subtract, [("cacc", i), mk_], [("cacc", i)])
                tt("dve", cacc[:, i, :], cacc[:, i, :], rc[:], ALU.mult, [("cacc", i), rck], [("cacc", i)])
                act(sbT[:, i, :], cacc[:, i, :], AF.Silu, [("cacc", i), ("vp", l)], [("sbT", i)],
                    bias=V[:, VP["lnb"] + i:VP["lnb"] + i + 1], scale=V[:, VP["lng"] + i:VP["lng"] + i + 1])
            S.stage("sc2")
            ssl, skey = wload([(dgs[l], 1, 12 * 128, 0, 12 * 128)], [("dgs", l)])
            for i in range(4):
                bk, bkey = bank()
                for k in range(3):
                    c_ = (i * 3 + k) * 128
                    mm(bk[:], ssl[:, c_:c_ + 128], p_raw[:, i, k:k + T], k == 0, k == 2,
                       [skey, ("praw", i), "praw_h"], [bkey])
                tt("dve", q_bf[:, i, :], bk[:], gb_sb[:, i, :], ALU.mult, [bkey, ("gb", i)], [("q", i)])
            cp("pool", halo_p[l][:], p_raw[:, :, T:T + 2], [("praw", i) for i in range(4)], [("hp", l)])
            S.stage("merge")
            for qt in range(4):
                c0 = qt * 256
                g1, g1k = wload([(wi[:, OFF_G + c0:OFF_G + c0 + 256], 8, 512, 0, 256),
                                 (wi[:, OFF_G + 1024 + c0:OFF_G + 1024 + c0 + 256], 8, 512, 256, 256)], wbk)
                g2, g2k = wload([(wi[:, OFF_G + 2048 + c0:OFF_G + 2048 + c0 + 256], 8, 256, 0, 256)], wbk)
                g1v, g2v = wview(g1, 8, 512), wview(g2, 8, 256)
                for jj in range(2):
                    j = qt * 2 + jj
                    for wch, (wv_, wk_, co) in enumerate(((g1v, g1k, jj * 128), (g1v, g1k, 256 + jj * 128), (g2v, g2k, jj * 128))):
                        bk, bkey = proj_cm(wv_, wk_, co, 8, hB, hBk)
                        bcol = VP["bg"] + wch * 8 + j
                        act(sig[:, wch * 2 + jj, :], bk[:], AF.Sigmoid, [bkey, ("vp", l)], [("sig", wch * 2 + jj)],
                            bias=V[:, bcol:bcol + 1])
                wa, wak = wload([(wbf["w_ssd_out"][l][:, c0:c0 + 256], 8, 256, 0, 256)], [("wb", "w_ssd_out", l)])
                wb_, wbk2 = wload([(wbf["w_conf_out"][l][:, c0:c0 + 256], 4, 256, 0, 256)], [("wb", "w_conf_out", l)])
                wc_, wck = wload([(wbf["w_sc_out"][l][:, c0:c0 + 256], 4, 256, 0, 256)], [("wb", "w_sc_out", l)])
                wav, wbv, wcv = wview(wa, 8, 256), wview(wb_, 4, 256), wview(wc_, 4, 256)
                for jj in range(2):
                    j = qt * 2 + jj
                    ba, bak = proj_cm(wav, wak, jj * 128, 8, lambda kc: bufA[:, kc, :], lambda kc: [("bufA", kc)])
                    bb, bbk = proj_cm(wbv, wbk2, jj * 128, 4, lambda kc: sbT[:, kc, :], lambda kc: [("sbT", kc)])
                    bc_, bck = proj_cm(wcv, wck, jj * 128, 4, lambda kc: q_bf[:, kc, :], lambda kc: [("q", kc)])
                    m1, m1k = work()
                    m2, m2k = work()
                    m3, m3k = work()
                    tt("dve", m1[:], ba[:], sig[:, 0 + jj, :], ALU.mult, [bak, ("sig", 0 + jj)], [m1k])
                    tt("dve", m2[:], bb[:], sig[:, 2 + jj, :], ALU.mult, [bbk, ("sig", 2 + jj)], [m2k])
                    tt("dve", m3[:], bc_[:], sig[:, 4 + jj, :], ALU.mult, [bck, ("sig", 4 + jj)], [m3k])
                    tt("pool", m1[:], m1[:], m2[:], ALU.add, [m1k, m2k], [m1k])
                    tt("pool", xbcs[:, j, :], m1[:], m3[:], ALU.add, [m1k, m3k], [("xbcs", j)])
            S.stage("wo")
            gate = mod[(l, "mix")]
            for ob in range(2):
                wo, wok = wload([(wbf["w_o"][l][:, ob * 512:(ob + 1) * 512], 8, 512, 0, 512)], [("wb", "w_o", l)])
                wov = wview(wo, 8, 512)
                for jj in range(4):
                    j = ob * 4 + jj
                    bk, bkey = proj_cm(wov, wok, jj * 128, 8, lambda kc: xbcs[:, kc, :], lambda kc: [("xbcs", kc)])
                    stt("dve", x_sb[:, j, :], bk[:], gate[:, 16 + j:17 + j], x_sb[:, j, :], ALU.mult, ALU.add,
                        [bkey, ("mod", l, "mix"), ("x", j)], [("x", j)])

        def ffn(l):
            V = vp[l]
            wu = wbf["w_up"][l]
            S.stage("ffn_h")
            make_h(l, "ffn")
            S.stage("ffn_up")
            gate = mod[(l, "ffn")]
            TL = tails[l]
            abuf = [a_bf, a_bf2]
            for half in range(2):
                AB = abuf[half]
                for pb in range(6):
                    i0 = half * 11 + pb * 2
                    nch = min(2, half * 11 + 11 - i0)
                    ncol = nch * 128
                    slot, wkey = wload([(wu[:, i0 * 128:i0 * 128 + ncol], 8, 512, 0, ncol),
                                        (wu[:, DFF + i0 * 128:DFF + i0 * 128 + ncol], 8, 512, 256, ncol)],
                                       [("wb", "w_up", l)])
                    wv = wview(slot, 8, 512)
                    ch = []
                    for ii in range(nch):
                        i = i0 + ii
                        for (co, ci) in ((ii * 128, i), (256 + ii * 128, 22 + i)):
                            bk, bkey = proj_cm(wv, wkey, co, 8, hB, hBk)
                            acc, akey = work()
                            ch.append((bk, bkey, acc, akey, ci, VP["fw"] + ci * 3))
                    for (bk, bkey, acc, akey, ci, fw) in ch:
                        act(acc[:], bk[:], AF.Identity, [bkey, ("vp", l)], [akey],
                            bias=V[:, VP["fb"] + ci:VP["fb"] + ci + 1], scale=V[:, fw + 2:fw + 3])
                    for (bk, bkey, acc, akey, ci, fw) in ch:
                        stt("dve", acc[:, 1:T], bk[:, 0:T - 1], V[:, fw + 1:fw + 2], acc[:, 1:T], ALU.mult, ALU.add,
                            [bkey, ("vp", l), akey], [akey])
                    for (bk, bkey, acc, akey, ci, fw) in ch:
                        stt("dve", acc[:, 2:T], bk[:, 0:T - 2], V[:, fw:fw + 1], acc[:, 2:T], ALU.mult, ALU.add,
                            [bkey, ("vp", l), akey], [akey])
                    for (bk, bkey, acc, akey, ci, fw) in ch:
                        stt("dve", acc[:, 0:2], TL[:, ci, 0:2], V[:, fw:fw + 1], acc[:, 0:2], ALU.mult, ALU.add,
                            [("tl", l, ci), ("tl", l), ("vp", l), akey], [akey])
                    for (bk, bkey, acc, akey, ci, fw) in ch:
                        stt("dve", acc[:, 0:1], TL[:, ci, 1:2], V[:, fw + 1:fw + 2], acc[:, 0:1], ALU.mult, ALU.add,
                            [("tl", l, ci), ("tl", l), ("vp", l), akey], [akey])
                    for (bk, bkey, acc, akey, ci, fw) in ch:
                        cp("act", TL[:, ci, :], bk[:, T - 2:T], [bkey, ("tl", l)], [("tl", l, ci)])
                    for ii in range(nch):
                        ag, agk = ch[2 * ii][2], ch[2 * ii][3]
                        act(ag[:], ag[:], AF.Silu, [agk], [agk])
                    for ii in range(nch):
                        i = i0 + ii
                        ag, agk = ch[2 * ii][2], ch[2 * ii][3]
                        av, avk = ch[2 * ii + 1][2], ch[2 * ii + 1][3]
                        tt("dve", AB[:, i - half * 11, :], av[:], ag[:], ALU.mult, [agk, avk], [("a_bf", half, i - half * 11)])
            wdn = wbf["w_down"][l]
            for half in range(2):
                AB = abuf[half]
                for jb in range(4):
                    slot, wkey = wload([(wdn[half * 11 * 128:(half + 1) * 11 * 128, jb * 256:(jb + 1) * 256], 11, 256, 0, 256)],
                                       [("wb", "w_down", l)])
                    wv = wview(slot, 11, 256)
                    for jj in range(2):
                        j = jb * 2 + jj
                        bk, bkey = proj_cm(wv, wkey, jj * 128, 11, (lambda AB_: (lambda kc: AB_[:, kc, :]))(AB),
                                           (lambda h_: (lambda kc: [("a_bf", h_, kc)]))(half))
                        stt("dve", x_sb[:, j, :], bk[:], gate[:, 16 + j:17 + j], x_sb[:, j, :], ALU.mult, ALU.add,
                            [bkey, ("mod", l, "ffn"), ("x", j)], [("x", j)])

        xT3 = xT.rearrange("(k p) t -> p k t", p=128)
        oT3 = outT.rearrange("(k p) t -> p k t", p=128)
        for ti in range(NT):
            t0 = ti * T
            S.op("sp", (lambda d, s: (lambda e: e.dma_start(out=d, in_=s)))(x_sb[:], xT3[:, :, t0:t0 + T]),
                 [], xkeys, dma=ch_x, force=True)
            for l in range(depth):
                mixer(l)
                ffn(l)
            S.stage("final")
            rms(depth - 1)
            gf = VP["gfin"]
            for kc in range(8):
                stt("dve", x_sb[:, kc, :], x_sb[:, kc, :], vp[0][:, gf + kc:gf + kc + 1], rstd[:], ALU.mult, ALU.mult,
                    [("x", kc), ("vp", 0), "rstd"], [("x", kc)])
            S.op("sp", (lambda d, s: (lambda e: e.dma_start(out=d, in_=s)))(oT3[:, :, t0:t0 + T], x_sb[:]),
                 xkeys, [("out", ti)], dma=ch_out, force=True)
        S.final_waits("sp")

        with nc.Block() as block:
            @block.tensor
            def _(e):
                S.emit("pe", e)

            @block.scalar
            def _(e):
                S.emit("act", e)

            @block.vector
            def _(e):
                S.emit("dve", e)

            @block.gpsimd
            def _(e):
                S.emit("pool", e)

            @block.sync
            def _(e):
                S.emit("sp", e)
    return nc


def make_packs(inp, depth=DEPTH):
    f = lambda a: np.asarray(a, dtype=np.float32)

    def cm(v):
        v = f(v)
        return v.reshape(-1, 128).T

    def cmk(w):
        w = f(w)
        K, C = w.shape
        return w.T.reshape(C // 128, 128, K).transpose(1, 0, 2).reshape(128, -1)

    vp = np.zeros((depth, 128, NV), np.float32)
    bp = np.zeros((depth, 128, NB), np.float32)
    for l in range(depth):
        def put(name, arr):
            vp[l, :, VP[name]:VP[name] + arr.shape[1]] = arr
        put("g_mix", cm(inp["norm_mix_g"][l]))
        put("g_ffn", cm(inp["norm_ffn_g"][l]))
        put("ab_mix", cm(inp["ada_mix_b"][l]))
        put("ab_ffn", cm(inp["ada_ffn_b"][l]))
        put("xw", cmk(inp["ssd_conv_w"][l]))
        put("xb", cm(inp["ssd_conv_b"][l]))
        put("cw", cmk(inp["conf_conv_w"][l]))
        put("cb", cm(inp["conf_conv_b"][l]))
        put("lng", cm(inp["conf_ln_g"][l]))
        put("lnb", cm(inp["conf_ln_b"][l]))
        put("sw", cmk(inp["sc_conv_w"][l]))
        put("bg", cm(inp["b_gate"][l]))
        put("ng", cm(inp["ssd_norm_g"][l]))
        put("fw", cmk(inp["ffn_conv_w"][l]))
        put("fb", cm(inp["ffn_conv_b"][l]))
        put("gfin", cm(inp["final_norm_g"]))
        bp[l, :, 0:16] = f(inp["ssd_dt_bias"][l])[None, :]
        bp[l, :, 16:32] = f(inp["ssd_a_log"][l])[None, :]
        bp[l, :, 32:48] = f(inp["ssd_d"][l])[None, :]
    return vp, bp


_NC_CACHE = {}


def kernel(**inputs):
    x = np.asarray(inputs["x"], dtype=np.float32)
    B, L, _ = x.shape
    depth = np.asarray(inputs["w_in"]).shape[0]
    key = (L, depth)
    if key not in _NC_CACHE:
        _NC_CACHE[key] = build_nc(L, depth)
    nc = _NC_CACHE[key]
    vp, bp = make_packs(inputs, depth)
    c = np.asarray(inputs["c"], dtype=np.float32)
    wnames = ["ada_mix_w", "ada_ffn_w", "w_in", "w_ssd_out", "w_conf_out", "w_sc_out", "w_o", "w_up", "w_down"]
    wts_ = {n: np.ascontiguousarray(np.asarray(inputs[n], dtype=np.float32)) for n in wnames}
    n_cores = int(os.environ.get("KCORES", "8"))
    in_maps = []
    active = {0: 0, 1: 1, 4: 2, 5: 3} if (n_cores == 8 and B == 4) else {i: (i * B) // n_cores for i in range(n_cores)}
    zero_w = None
    for i in range(n_cores):
        if i in active:
            b = active[i]
            m = {"xT": np.ascontiguousarray(x[b].T), "cpk": np.ascontiguousarray(c[b].reshape(8, 128).T),
                 "vp": vp, "bp": bp}
            m.update(wts_)
        else:
            if zero_w is None:
                zero_w = {n: np.zeros_like(a) for n, a in wts_.items()}
                zero_w.update({"xT": np.zeros((D, L), np.float32), "cpk": np.zeros((128, 8), np.float32),
                               "vp": np.zeros_like(vp), "bp": np.zeros_like(bp)})
            m = dict(zero_w)
        in_maps.append(m)
    res = run_bass_kernel_spmd(nc, in_maps, core_ids=list(range(n_cores)))
    out = np.empty((B, L, D), np.float32)
    inv = {}
    for i, b in active.items():
        inv.setdefault(b, i)
    for b in range(B):
        i = inv.get(b, 0)
        out[b] = np.asarray(res.results[i]["outT"]).T
    return out
```

```python
import os
import numpy as np
from contextlib import ExitStack
import concourse.bass as bass
import concourse.mybir as mybir
from concourse.bass_utils import run_bass_kernel_spmd

F32 = mybir.dt.float32
BF16 = mybir.dt.bfloat16
AF = mybir.ActivationFunctionType
ALU = mybir.AluOpType
AX = mybir.AxisListType

D = 1024
DEPTH = 2
NIN = 7952
DFF = 2816
EPS = 1e-6
T = 512
OFF_XBC = 1024
OFF_DT = 2304
OFF_CONF = 2320
OFF_SC = 3344
OFF_G = 4880

VP = {}
_o = 0
for _n, _w in [("g_mix", 8), ("g_ffn", 8), ("ab_mix", 24), ("ab_ffn", 24), ("xw", 40), ("xb", 10),
               ("cw", 124), ("cb", 4), ("lng", 4), ("lnb", 4), ("sw", 12), ("bg", 24), ("ng", 8),
               ("fw", 132), ("fb", 44), ("gfin", 8)]:
    VP[_n] = _o
    _o += _w
NV = _o
NB = 16 + 16 + 16


def bc(ap, pos, n):
    l = [list(x) for x in ap.ap]
    l.insert(pos, [0, n])
    return bass.AP(ap.tensor, ap.offset, l)


def bclast(ap, n):
    l = [list(x) for x in ap.ap]
    assert l[-1][1] == 1
    l[-1] = [0, n]
    return bass.AP(ap.tensor, ap.offset, l)


class Sched:
    ENG = ("pe", "act", "dve", "pool", "sp")

    def __init__(self, nc, es):
        self.nc = nc
        self.es = es
        self.ops = {n: [] for n in self.ENG}
        self.count = {n: 0 for n in self.ENG}
        self.seen = {n: {} for n in self.ENG}
        self.sems = {n: es.enter_context(nc.semaphore("s_" + n)) for n in self.ENG}
        self.res = {}
        self.chans = []
        self.nbank = 0

    def chan(self, name):
        sem = self.es.enter_context(self.nc.semaphore("c_" + name))
        self.chans.append({"sem": sem, "count": 0})
        return len(self.chans) - 1

    def _r(self, k):
        r = self.res.get(k)
        if r is None:
            r = {"w": None, "rd": {}}
            self.res[k] = r
        return r

    def stage(self, name):
        self.nstage = getattr(self, "nstage", 0) + 1
        lim = int(os.environ.get("KSTAGE", "0"))
        if lim and self.nstage > lim and not getattr(self, "muted", False):
            self.muted = True
            print("MUTED before stage", self.nstage, name)

    def op(self, eng, fn, reads=(), writes=(), dma=None, force=False):
        if getattr(self, "muted", False) and not force:
            return
        waits = {}
        seen = self.seen[eng]

        def need(ev):
            if ev is None:
                return
            kind, who, val = ev
            if kind == "e" and who == eng and eng == "pe":
                return
            key = (kind, who)
            if seen.get(key, 0) >= val:
                return
            if waits.get(key, 0) < val:
                waits[key] = val

        for k in reads:
            r = self._r(k)
            need(r["w"])
            if isinstance(k, tuple) and k[0] == "ps":
                for ev in r["rd"].values():
                    if ev[1] != eng:
                        need(ev)
        for k in writes:
            r = self._r(k)
            need(r["w"])
            for ev in r["rd"].values():
                need(ev)
        for key, val in waits.items():
            seen[key] = val
        if dma is None:
            self.count[eng] += 1
            ev = ("e", eng, self.count[eng])
        else:
            c = self.chans[dma]
            c["count"] += 16
            ev = ("d", dma, c["count"])
        for k in reads:
            self._r(k)["rd"][(ev[0], ev[1])] = ev
        for k in writes:
            r = self._r(k)
            r["w"] = ev
            r["rd"] = {}
        self.ops[eng].append((waits, fn, ev))

    def final_waits(self, eng):
        waits = {}
        for i, c in enumerate(self.chans):
            if c["count"] > 0:
                waits[("d", i)] = c["count"]
        self.ops[eng].append((waits, None, None))

    def emit(self, name, e):
        for waits, fn, ev in self.ops[name]:
            for (kind, who), val in waits.items():
                sem = self.sems[who] if kind == "e" else self.chans[who]["sem"]
                e.wait_ge(sem, val)
            if fn is None:
                continue
            ins = fn(e)
            if ev[0] == "e":
                ins.then_inc(self.sems[name], 1)
            else:
                ins.then_inc(self.chans[ev[1]]["sem"], 16)


def build_nc(L, depth=DEPTH):
    NT = L // T
    nc = bass.Bass("TRN2", target_bir_lowering=False)
    dt_ = nc.dram_tensor
    xT = dt_("xT", [D, L], F32, kind="ExternalInput").ap()
    cpk = dt_("cpk", [128, 8], F32, kind="ExternalInput").ap()
    vp_d = dt_("vp", [depth, 128, NV], F32, kind="ExternalInput").ap()
    bp_d = dt_("bp", [depth, 128, NB], F32, kind="ExternalInput").ap()
    wsrc = {}
    wshape = {"ada_mix_w": [D, 3 * D], "ada_ffn_w": [D, 3 * D], "w_in": [D, NIN], "w_ssd_out": [D, D],
              "w_conf_out": [512, D], "w_sc_out": [512, D], "w_o": [D, D], "w_up": [D, 2 * DFF],
              "w_down": [DFF, D]}
    wbf = {}
    for n, shp in wshape.items():
        wsrc[n] = dt_(n, [depth] + shp, F32, kind="ExternalInput").ap()
        wbf[n] = dt_(n + "_bf", [depth] + shp, BF16, kind="Internal").ap()
    outT = dt_("outT", [D, L], F32, kind="ExternalOutput").ap()
    dgw = dt_("dgw", [depth, 4, 128, 31 * 128], BF16, kind="Internal").ap()
    dgx = dt_("dgx", [depth, 2, 128, 20 * 128], BF16, kind="Internal").ap()
    dgs = dt_("dgs", [depth, 128, 12 * 128], BF16, kind="Internal").ap()

    es = ExitStack()
    with es:
        S = Sched(nc, es)

        def sb(name, shape, dtype):
            return es.enter_context(nc.sbuf_tensor(name, shape, dtype))

        x_sb = sb("x_sb", [128, 8, T], F32)
        bufA = sb("bufA", [128, 8, T], BF16)
        bufB = sb("bufB", [128, 8, T], BF16)
        rstd = sb("rstd", [128, T], F32)
        NWK = 5
        wk = [sb(f"wk{i}", [128, T], F32) for i in range(NWK)]
        NW = 5
        wslot = [sb(f"ws{i}", [128, 4096], BF16) for i in range(NW)]
        wchan = [S.chan(f"w{i}") for i in range(NW)]
        Sst = [sb(f"S{l}", [128, 512], F32) for l in range(depth)]
        Sbf = [sb(f"Sb{l}", [128, 512], BF16) for l in range(depth)]
        xbc_raw = sb("xbc_raw", [128, 10, 3 + T], BF16)
        xbcs = sb("xbcs", [128, 10, T], BF16)
        u_raw = sb("u_raw", [128, 4, 30 + T], BF16)
        cacc = sb("cacc", [128, 4, T], F32)
        sbT = sb("sbT", [128, 4, T], BF16)
        p_raw = sb("p_raw", [128, 4, 2 + T], BF16)
        gb_sb = sb("gb_sb", [128, 4, T], BF16)
        q_bf = sb("q_bf", [128, 4, T], BF16)
        sig = sb("sig", [128, 6, T], BF16)
        a_bf = sb("a_bf", [128, 11, T], BF16)
        a_bf2 = sb("a_bf2", [128, 11, T], BF16)
        halo_x = [sb(f"hx{l}", [128, 10, 3], BF16) for l in range(depth)]
        halo_u = [sb(f"hu{l}", [128, 4, 30], BF16) for l in range(depth)]
        halo_p = [sb(f"hp{l}", [128, 4, 2], BF16) for l in range(depth)]
        tails = [sb(f"tl{l}", [128, 44, 2], F32) for l in range(depth)]
        zsb = [sb(f"zs{i}", [128, 1024], BF16) for i in range(2)]
        decT = [sb(f"decT{i}", [128, 4, 128], BF16) for i in range(2)]
        wts = sb("wts", [128, 16, 128], BF16)
        cbt = sb("cbt", [128, 2, 128], BF16)
        xs_tokb = [sb(f"xs_tok{i}", [128, 1024], BF16) for i in range(2)]
        xc = sb("xc", [128, 1024], BF16)
        xd = sb("xd", [128, 1024], BF16)
        b_tok = sb("b_tok", [128, 128], BF16)
        t1 = sb("t1", [128, 1024], F32)
        t2 = sb("t2", [128, 1024], F32)
        vn = sb("vn", [128, 1024], BF16)
        dtx = sb("dtx", [128, 64], F32)
        dta = sb("dta", [128, 64], F32)
        dte = sb("dte", [128, 64], F32)
        dtp = sb("dtp", [128, 64], F32)
        adt = sb("adt", [128, 64], F32)
        ahl = sb("ahl", [128, 128], BF16)
        nacum = sb("nacum", [128, 64], F32)
        eacum = sb("eacum", [128, 64], F32)
        acl = sb("acl", [128, 16], F32)
        dsx = sb("dsx", [128, 16], F32)
        ds = sb("ds", [128, 16], F32)
        cd = sb("cd", [128, 16], F32)
        ssq = sb("ssq", [128, 2], F32)
        rsv = sb("rsv", [128, 2], F32)
        ident = sb("ident", [128, 128], BF16)
        tri = sb("tri", [128, 128], BF16)
        mneg = sb("mneg", [128, 128], BF16)
        ones_bf = sb("ones_bf", [128, 128], BF16)
        ones_f = sb("ones_f", [128, 128], F32)
        wconst = sb("wconst", [128, 512], BF16)
        vp = [sb(f"vp{l}", [128, NV], F32) for l in range(depth)]
        bp = [sb(f"bp{l}", [128, NB], F32) for l in range(depth)]
        arep = [sb(f"arep{l}", [128, 16], F32) for l in range(depth)]
        mod = {(l, w): sb(f"mod{l}{w}", [128, 24], F32) for l in range(depth) for w in ("mix", "ffn")}
        gsv = {(l, w): sb(f"gs{l}{w}", [128, 8], F32) for l in range(depth) for w in ("mix", "ffn")}
        c_sb = sb("c_sb", [128, 8], F32)
        c_sig = sb("c_sig", [128, 8], F32)
        c_bf = sb("c_bf", [128, 8], BF16)
        banks = [es.enter_context(nc.psum_tensor(f"pb{i}", [128, 512], F32)) for i in range(8)]

        ch_misc = S.chan("misc")
        ch_x = S.chan("x")
        ch_out = S.chan("out")

        def bank():
            i = S.nbank % 7
            S.nbank += 1
            return banks[i], ("ps", i)

        def warm(n):
            for _ in range(n):
                mm(banks[7][:], ones_bf[:], wconst[:], True, True, ["const"], [("ps", 7)])

        nwk = [0]

        def work():
            i = nwk[0] % NWK
            nwk[0] += 1
            return wk[i], ("wk", i)

        nws = [0]

        def wload(pieces, key_reads):
            i = nws[0] % NW
            nws[0] += 1
            slot = wslot[i]
            for (src, nkc, width, coff, ncols) in pieces:
                dst = slot[:, 0:nkc * width].rearrange("p (k c) -> p k c", k=nkc)[:, :, coff:coff + ncols]
                s3 = src.rearrange("(k p) c -> p k c", p=128)
                S.op("sp", (lambda d, s: (lambda e: e.dma_start(out=d, in_=s)))(dst, s3),
                     reads=key_reads, writes=[("ws", i)], dma=wchan[i])
            return slot, ("ws", i)

        def wview(slot, nkc, width):
            return slot[:, 0:nkc * width].rearrange("p (k c) -> p k c", k=nkc)

        def mm(out, lhsT, rhs, start, stop, reads, writes):
            S.op("pe", lambda e: e.matmul(out, lhsT, rhs, start=start, stop=stop), reads, writes)

        def tr(out, in_, reads, writes):
            S.op("pe", lambda e: e.transpose(out, in_, ident[:]), reads + ["const"], writes)

        def act(out, in_, func, reads, writes, bias=None, scale=None):
            kw = {}
            if bias is not None:
                kw["bias"] = bias
            if scale is not None:
                kw["scale"] = scale
            S.op("act", lambda e: e.activation(out=out, in_=in_, func=func, **kw), reads, writes)

        def tt(eng, out, in0, in1, op, reads, writes):
            S.op(eng, lambda e: e.tensor_tensor(out=out, in0=in0, in1=in1, op=op), reads, writes)

        def ts(eng, out, in0, s1, s2, op0, op1, reads, writes):
            if op1 is None:
                S.op(eng, lambda e: e.tensor_scalar(out=out, in0=in0, scalar1=s1, scalar2=None, op0=op0), reads, writes)
            else:
                S.op(eng, lambda e: e.tensor_scalar(out=out, in0=in0, scalar1=s1, scalar2=s2, op0=op0, op1=op1), reads, writes)

        def stt(eng, out, in0, scalar, in1, op0, op1, reads, writes):
            S.op(eng, lambda e: e.scalar_tensor_tensor(out=out, in0=in0, scalar=scalar, in1=in1, op0=op0, op1=op1),
                 reads, writes)

        def cp(eng, out, in_, reads, writes):
            if eng == "act":
                S.op(eng, lambda e: e.activation(out=out, in_=in_, func=AF.Copy), reads, writes)
            else:
                S.op(eng, lambda e: e.tensor_copy(out=out, in_=in_), reads, writes)

        def rsqrt_inplace(ap, key, scale):
            ts("dve", ap, ap, scale, EPS, ALU.mult, ALU.add, [key], [key])
            act(ap, ap, AF.Ln, [key], [key])
            act(ap, ap, AF.Exp, [key], [key], scale=-0.5)

        cast_order = []
        for l in range(depth):
            cast_order += [("ada_mix_w", l), ("ada_ffn_w", l)]
        for l in range(depth):
            cast_order += [("w_in", l), ("w_ssd_out", l), ("w_conf_out", l), ("w_sc_out", l), ("w_o", l),
                           ("w_up", l), ("w_down", l)]
        for (n, l) in cast_order:
            rows = wshape[n][0]
            step = 256
            ch_cast = S.chan("cast_%s%d" % (n, l))
            for r0 in range(0, rows, step):
                r1 = min(rows, r0 + step)
                S.op("pool", (lambda d, s: (lambda e: e.dma_start(out=d, in_=s)))(wbf[n][l, r0:r1, :], wsrc[n][l, r0:r1, :]),
                     reads=[], writes=[("wb", n, l)], dma=ch_cast)

        S.op("pool", lambda e: e.memset(ident[:], 1.0), [], ["const"])
        S.op("pool", lambda e: e.affine_select(out=ident[:], in_=ident[:], pattern=[[1, 128]], compare_op=ALU.is_equal,
                                               fill=0.0, base=0, channel_multiplier=-1), [], ["const"])
        S.op("pool", lambda e: e.memset(tri[:], 1.0), [], ["const"])
        S.op("pool", lambda e: e.affine_select(out=tri[:], in_=tri[:], pattern=[[1, 128]], compare_op=ALU.is_ge,
                                               fill=0.0, base=0, channel_multiplier=-1), [], ["const"])
        S.op("pool", lambda e: e.memset(mneg[:], 0.0), [], ["const"])
        S.op("pool", lambda e: e.affine_select(out=mneg[:], in_=mneg[:], pattern=[[1, 128]], compare_op=ALU.is_ge,
                                               fill=-30000.0, base=0, channel_multiplier=-1), [], ["const"])
        S.op("pool", lambda e: e.memset(ones_bf[:], 1.0), [], ["const"])
        S.op("pool", lambda e: e.memset(ones_f[:], 1.0), [], ["const"])
        S.op("pool", lambda e: e.memset(wconst[:], 0.5), [], ["const"])
        for l in range(depth):
            S.op("pool", (lambda a: (lambda e: e.memset(a, 0.0)))(Sst[l][:]), [], [("S", l)])
            S.op("pool", (lambda a: (lambda e: e.memset(a, 0.0)))(Sbf[l][:]), [], [("Sb", l)])
            S.op("pool", (lambda a: (lambda e: e.memset(a, 0.0)))(halo_x[l][:]), [], [("hx", l)])
            S.op("pool", (lambda a: (lambda e: e.memset(a, 0.0)))(halo_u[l][:]), [], [("hu", l)])
            S.op("pool", (lambda a: (lambda e: e.memset(a, 0.0)))(halo_p[l][:]), [], [("hp", l)])
            S.op("pool", (lambda a: (lambda e: e.memset(a, 0.0)))(tails[l][:]), [], [("tl", l)])
            S.op("sp", (lambda d, s: (lambda e: e.dma_start(out=d, in_=s)))(vp[l][:], vp_d[l]), [], [("vp", l)], dma=S.chan("vp%d" % l))
            S.op("sp", (lambda d, s: (lambda e: e.dma_start(out=d, in_=s)))(bp[l][:], bp_d[l]), [], [("bp", l)], dma=S.chan("bp%d" % l))
        S.op("sp", lambda e: e.dma_start(out=c_sb[:], in_=cpk), [], ["c"], dma=ch_misc)
        act(c_sig[:], c_sb[:], AF.Sigmoid, ["c"], ["csig"])
        tt("dve", c_bf[:], c_sb[:], c_sig[:], ALU.mult, ["c", "csig"], ["cbf"])
        for l in range(depth):
            act(arep[l][:], bp[l][:, 16:32], AF.Exp, [("bp", l)], [("arep", l)])
            ts("dve", arep[l][:], arep[l][:], -1.0, None, ALU.mult, None, [("arep", l)], [("arep", l)])
            for w in ("mix", "ffn"):
                wn = "ada_%s_w" % w
                bk, bkey = bank()
                for blk in range(6):
                    slot, wkey = wload([(wbf[wn][l, :, blk * 512:(blk + 1) * 512], 8, 512, 0, 512)], [("wb", wn, l)])
                    wv = wview(slot, 8, 512)
                    for oc4 in range(4):
                        oc = blk * 4 + oc4
                        for kc in range(8):
                            mm(bk[:, oc:oc + 1], wv[:, kc, oc4 * 128:(oc4 + 1) * 128], c_bf[:, kc:kc + 1],
                               kc == 0, kc == 7, [wkey, "cbf"], [bkey])
                ab = VP["ab_" + w]
                tt("dve", mod[(l, w)][:], bk[:, 0:24], vp[l][:, ab:ab + 24], ALU.add, [bkey, ("vp", l)], [("mod", l, w)])
                g0 = VP["g_" + w]
                stt("dve", gsv[(l, w)][:], mod[(l, w)][:, 8:16], 1.0, vp[l][:, g0:g0 + 8], ALU.add, ALU.mult,
                    [("mod", l, w), ("vp", l)], [("gs", l, w)])

        for l in range(depth):
            for i in range(4):
                si = nws[0] % NW
                nws[0] += 1
                cw = VP["cw"] + i * 31
                for k in range(31):
                    ts("dve", wslot[si][:, k * 128:(k + 1) * 128], ident[:], vp[l][:, cw + k:cw + k + 1], None, ALU.mult, None,
                       ["const", ("vp", l)], [("ws", si)])
                S.op("sp", (lambda d, s_: (lambda e: e.dma_start(out=d, in_=s_)))(dgw[l, i], wslot[si][:, 0:31 * 128]),
                     [("ws", si)], [("dgw", l, i)], dma=S.chan("dgw%d_%d" % (l, i)))
        for l in range(depth):
            for gi in range(2):
                si = nws[0] % NW
                nws[0] += 1
                for cj in range(5):
                    xw = VP["xw"] + (gi * 5 + cj) * 4
                    for k in range(4):
                        c_ = (cj * 4 + k) * 128
                        ts("dve", wslot[si][:, c_:c_ + 128], ident[:], vp[l][:, xw + k:xw + k + 1], None, ALU.mult, None,
                           ["const", ("vp", l)], [("ws", si)])
                S.op("sp", (lambda d, s_: (lambda e: e.dma_start(out=d, in_=s_)))(dgx[l, gi], wslot[si][:, 0:20 * 128]),
                     [("ws", si)], [("dgx", l, gi)], dma=S.chan("dgx%d_%d" % (l, gi)))
            si = nws[0] % NW
            nws[0] += 1
            for i in range(4):
                sw = VP["sw"] + i * 3
                for k in range(3):
                    c_ = (i * 3 + k) * 128
                    ts("dve", wslot[si][:, c_:c_ + 128], ident[:], vp[l][:, sw + k:sw + k + 1], None, ALU.mult, None,
                       ["const", ("vp", l)], [("ws", si)])
            S.op("sp", (lambda d, s_: (lambda e: e.dma_start(out=d, in_=s_)))(dgs[l], wslot[si][:, 0:12 * 128]),
                 [("ws", si)], [("dgs", l)], dma=S.chan("dgs%d" % l))
        xkeys = [("x", j) for j in range(8)]
        Akeys = [("bufA", j) for j in range(8)]
        Bkeys = [("bufB", j) for j in range(8)]

        def rms(l):
            bk, bkey = bank()
            for kc in range(8):
                act(bufA[:, kc, :], x_sb[:, kc, :], AF.Square, [("x", kc)], [("bufA", kc)])
                mm(bk[:], ones_bf[:], bufA[:, kc, :], kc == 0, kc == 7, [("bufA", kc), "const"], [bkey])
            ts("dve", rstd[:], bk[:], 1.0 / D, EPS, ALU.mult, ALU.add, [bkey], ["rstd"])
            act(rstd[:], rstd[:], AF.Ln, ["rstd"], ["rstd"])
            act(rstd[:], rstd[:], AF.Exp, ["rstd"], ["rstd"], scale=-0.5)

        def make_h(l, w):
            rms(l)
            warm(48)
            sh = mod[(l, w)]
            for kc in range(8):
                wt_, wkk = work()
                stt("dve", wt_[:], x_sb[:, kc, :], gsv[(l, w)][:, kc:kc + 1], rstd[:], ALU.mult, ALU.mult,
                    [("x", kc), ("gs", l, w), "rstd"], [wkk])
                act(bufB[:, kc, :], wt_[:], AF.Identity, [wkk, ("mod", l, w)], [("bufB", kc)], bias=sh[:, kc:kc + 1])

        def proj_cm(wv, wkey, c0, nk, rhs_fn, rhs_keys):
            bk, bkey = bank()
            for kc in range(nk):
                mm(bk[:], wv[:, kc, c0:c0 + 128], rhs_fn(kc), kc == 0, kc == nk - 1, [wkey] + rhs_keys(kc), [bkey])
            return bk, bkey

        hB = (lambda kc: bufB[:, kc, :])
        hBk = (lambda kc: [("bufB", kc)])

        def mixer(l):
            V = vp[l]
            wi = wbf["w_in"][l]
            wbk = [("wb", "w_in", l)]
            S.stage("mix_h")
            make_h(l, "mix")
            S.stage("xbc")
            wd, wdk = wload([(wi[:, OFF_DT:OFF_DT + 16], 8, 16, 0, 16)], wbk)
            wdv = wview(wd, 8, 16)
            BP = bp[l]
            bkd, bkdk = bank()
            for q in range(4):
                qs = slice(q * 128, (q + 1) * 128)
                for kc in range(8):
                    mm(bkd[:, q * 16:(q + 1) * 16], bufB[:, kc, qs], wdv[:, kc, :], kc == 0, kc == 7, [wdk, ("bufB", kc)], [bkdk])
            q4 = lambda ap: ap.rearrange("p (q h) -> p q h", q=4)
            tt("dve", q4(dtx[:]), q4(bkd[:, 0:64]), bc(BP[:, 0:16], 1, 4), ALU.add, [bkdk, ("bp", l)], ["dtx"])
            act(dta[:], dtx[:], AF.Abs, ["dtx"], ["dta"])
            act(dte[:], dta[:], AF.Exp, ["dta"], ["dte"], scale=-1.0)
            act(dte[:], dte[:], AF.Ln, ["dte"], ["dte"], bias=1.0)
            stt("dve", dtp[:], dtx[:], 0.0, dte[:], ALU.max, ALU.add, ["dtx", "dte"], ["dtp"])
            tt("dve", q4(adt[:]), q4(dtp[:]), bc(arep[l][:], 1, 4), ALU.mult, ["dtp", ("arep", l)], ["adt"])
            cp("dve", ahl[:, 0:64], adt[:], ["adt"], ["ahl"])
            tt("dve", ahl[:, 64:128], adt[:], ahl[:, 0:64], ALU.subtract, ["adt", "ahl"], ["ahl"])
            cp("pool", xbc_raw[:, :, 0:3], halo_x[l][:], [("hx", l)], ["xraw_h"])
            for (b0, ncol) in ((0, 512), (512, 512), (1024, 256)):
                slot, wkey = wload([(wi[:, OFF_XBC + b0:OFF_XBC + b0 + ncol], 8, 512, 0, ncol)], wbk)
                wv = wview(slot, 8, 512)
                for i in range(ncol // 128):
                    ci = b0 // 128 + i
                    bk, bkey = proj_cm(wv, wkey, i * 128, 8, hB, hBk)
                    cp("act", xbc_raw[:, ci, 3:3 + T], bk[:], [bkey], [("xraw", ci)])
            S.stage("conf1")
            cp("pool", u_raw[:, :, 0:30], halo_u[l][:], [("hu", l)], ["uraw_h"])
            sv, svk = wload([(wi[:, OFF_CONF:OFF_CONF + 512], 8, 512, 0, 512)], wbk)
            sg_, sgk = wload([(wi[:, OFF_CONF + 512:OFF_CONF + 1024], 8, 512, 0, 512)], wbk)
            svv, sgv = wview(sv, 8, 512), wview(sg_, 8, 512)
            for i in range(4):
                bv, bvk = proj_cm(svv, svk, i * 128, 8, hB, hBk)
                bg, bgk = proj_cm(sgv, sgk, i * 128, 8, hB, hBk)
                sgm, sgmk = work()
                act(sgm[:], bg[:], AF.Sigmoid, [bgk], [sgmk])
                tt("dve", u_raw[:, i, 30:30 + T], bv[:], sgm[:], ALU.mult, [bvk, sgmk], [("uraw", i)])
            S.stage("sc1")
            cp("pool", p_raw[:, :, 0:2], halo_p[l][:], [("hp", l)], ["praw_h"])
            wsl = [wload([(wi[:, OFF_SC + j * 512:OFF_SC + (j + 1) * 512], 8, 512, 0, 512)], wbk) for j in range(3)]
            wvv = [wview(s_, 8, 512) for (s_, _) in wsl]
            for i in range(4):
                bgb, bgbk = proj_cm(wvv[0], wsl[0][1], i * 128, 8, hB, hBk)
                cp("act", gb_sb[:, i, :], bgb[:], [bgbk], [("gb", i)])
                bgc, bgck = proj_cm(wvv[1], wsl[1][1], i * 128, 8, hB, hBk)
                gcs, gck = work()
                cp("act", gcs[:], bgc[:], [bgck], [gck])
                bxv, bxvk = proj_cm(wvv[2], wsl[2][1], i * 128, 8, hB, hBk)
                tt("dve", p_raw[:, i, 2:2 + T], bxv[:], gcs[:], ALU.mult, [bxvk, gck], [("praw", i)])
            for gi in range(2):
                xsl, xkey = wload([(dgx[l, gi], 1, 20 * 128, 0, 20 * 128)], [("dgx", l, gi)])
                for cj in range(5):
                    ci = gi * 5 + cj
                    bk, bkey = bank()
                    for k in range(4):
                        c_ = (cj * 4 + k) * 128
                        mm(bk[:], xsl[:, c_:c_ + 128], xbc_raw[:, ci, k:k + T], k == 0, k == 3,
                           [xkey, ("xraw", ci), "xraw_h"], [bkey])
                    act(xbcs[:, ci, :], bk[:], AF.Silu, [bkey, ("vp", l)], [("xbcs", ci)],
                        bias=V[:, VP["xb"] + ci:VP["xb"] + ci + 1])
            cp("pool", halo_x[l][:], xbc_raw[:, :, T:T + 3], [("xraw", ci) for ci in range(10)], [("hx", l)])
            def conf_conv_pe(i):
                dsl, dkey = wload([(dgw[l, i], 1, 31 * 128, 0, 31 * 128)], [("dgw", l, i)])
                bk, bkey = bank()
                for k in range(31):
                    mm(bk[:], dsl[:, k * 128:(k + 1) * 128], u_raw[:, i, k:k + T], k == 0, k == 30,
                       [dkey, ("uraw", i), "uraw_h"], [bkey])
                act(cacc[:, i, :], bk[:], AF.Identity, [bkey, ("vp", l)], [("cacc", i)],
                    bias=V[:, VP["cb"] + i:VP["cb"] + i + 1])

            pend = []

            def fill(n=1):
                for _ in range(n):
                    if not pend:
                        return
                    k, i = pend.pop(0)
                    cw = VP["cw"] + i * 31
                    stt("dve", cacc[:, i, :], u_raw[:, i, k:k + T], V[:, cw + k:cw + k + 1], cacc[:, i, :], ALU.mult, ALU.add,
                        [("uraw", i), "uraw_h", ("vp", l), ("cacc", i)], [("cacc", i)])
            S.stage("ssd")
            z0, zk0 = wload([(wi[:, 0:512], 8, 512, 0, 512)], wbk)
            z1, zk1 = wload([(wi[:, 512:1024], 8, 512, 0, 512)], wbk)
            zv = [wview(z0, 8, 512), wview(z1, 8, 512)]
            zk = [zk0, zk1]
            bka, bkak = bank()
            for q in range(4):
                mm(bka[:, q * 16:(q + 1) * 16], tri[:], ahl[:, q * 16:(q + 1) * 16], True, False, ["const", "ahl"], [bkak])
                mm(bka[:, q * 16:(q + 1) * 16], tri[:], ahl[:, 64 + q * 16:64 + (q + 1) * 16], False, True, ["const", "ahl"], [bkak])
            act(nacum[:], bka[:, 0:64], AF.Copy, [bkak], ["nacum"], scale=-1.0)
            act(eacum[:], bka[:, 0:64], AF.Exp, [bkak], ["eacum"])
            def front(q):
                qs = slice(q * 128, (q + 1) * 128)
                zq = zsb[q % 2]
                xq = xs_tokb[q % 2]
                for zb in range(2):
                    bk, bkey = bank()
                    for kc in range(8):
                        mm(bk[:], bufB[:, kc, qs], zv[zb][:, kc, :], kc == 0, kc == 7, [zk[zb], ("bufB", kc)], [bkey])
                    act(zq[:, zb * 512:(zb + 1) * 512], bk[:], AF.Silu, [bkey], [("zs", q % 2, zb)])
                    yield
                conf_conv_pe(q)
                yield
                for g in range(2):
                    gp = slice(g * 64, (g + 1) * 64)
                    bk, bkey = bank()
                    mm(bk[:, 0:128], xbcs[gp, 8, qs], xbcs[gp, 9, qs], True, True, [("xbcs", 8), ("xbcs", 9)], [bkey])
                    cp("act", cbt[:, g, :], bk[:, 0:128], [bkey], ["cbt"])
                yield
                for hq in range(4):
                    bk, bkey = bank()
                    for i in range(4):
                        h = hq * 4 + i
                        o = bk[:, i * 128:(i + 1) * 128]
                        mm(o, bclast(ahl[:, q * 16 + h:q * 16 + h + 1], 128), tri[:], True, False, ["ahl", "const"], [bkey])
                        mm(o, bclast(ahl[:, 64 + q * 16 + h:64 + q * 16 + h + 1], 128), tri[:], False, False, ["ahl", "const"], [bkey])
                        mm(o, ident[:], mneg[:], False, True, ["const"], [bkey])
                    dT = decT[hq % 2]
                    dk = ("decT", hq % 2)
                    for i in range(4):
                        h = hq * 4 + i
                        act(dT[:, i, :], bk[:, i * 128:(i + 1) * 128], AF.Exp, [bkey, "nacum"], [dk],
                            bias=nacum[:, q * 16 + h:q * 16 + h + 1])
                    cp("dve", acl[:, hq * 4:(hq + 1) * 4], bk[:, 127::128], [bkey], ["acl"])
                    tt("dve", wts[:, hq * 4:(hq + 1) * 4, :], dT[:], bc(cbt[:, hq // 2, :], 1, 4), ALU.mult,
                       [dk, "cbt"], [("wts", hq)])
                    yield
                tt("dve", dsx[:], acl[:], nacum[:, q * 16:(q + 1) * 16], ALU.add, ["acl", "nacum"], ["dsx"])
                act(ds[:], dsx[:], AF.Exp, ["dsx"], ["ds"])
                act(cd[:], acl[:], AF.Exp, ["acl"], ["cd"])
                yield
                bk, bkey = bank()
                bkb = bk[:].bitcast(BF16)
                for c in range(8):
                    tr(bkb[:, c * 128:(c + 1) * 128], xbcs[:, c, qs], [("xbcs", c)], [bkey])
                cp("act", xq[:], bkb, [bkey], [("xs_tok", q % 2)])
                tt("dve", xc[:].rearrange("p (h d) -> p h d", h=16), bkb.rearrange("p (h d) -> p h d", h=16),
                   bc(dtp[:, q * 16:(q + 1) * 16], 2, 64), ALU.mult, [bkey, "dtp"], ["xc"])
                yield
                tt("pool", xd[:].rearrange("p (h d) -> p h d", h=16), xc[:].rearrange("p (h d) -> p h d", h=16),
                   bc(ds[:], 2, 64), ALU.mult, ["xc", "ds"], ["xd"])
                bk, bkey = bank()
                bkb2 = bk[:].bitcast(BF16)
                tr(bkb2[:, 0:128], xbcs[:, 8, qs], [("xbcs", 8)], [bkey])
                cp("act", b_tok[:], bkb2[:, 0:128], [bkey], ["b_tok"])
                yield

            def mid(q):
                qs = slice(q * 128, (q + 1) * 128)
                by = [bank(), bank()]
                for h in range(16):
                    b_, bk_ = by[h // 8]
                    mm(b_[:, (h % 8) * 64:(h % 8 + 1) * 64], wts[:, h, :], xc[:, h * 64:(h + 1) * 64], True, True,
                       [("wts", h // 4), "xc"], [bk_])
                bo = [bank(), bank()]
                for g in range(2):
                    gp = slice(g * 64, (g + 1) * 64)
                    mm(bo[g][0][:], xbcs[gp, 9, qs], Sbf[l][gp, :], True, True, [("xbcs", 9), ("Sb", l)], [bo[g][1]])
                for g in range(2):
                    hs = slice(g * 512, (g + 1) * 512)
                    tt("dve", t1[:, hs].rearrange("p (h d) -> p h d", h=8), bo[g][0][:].rearrange("p (h d) -> p h d", h=8),
                       bc(eacum[:, q * 16 + g * 8:q * 16 + (g + 1) * 8], 2, 64), ALU.mult, [bo[g][1], "eacum"], [("t1", g)])
                    tt("dve", t1[:, hs], t1[:, hs], by[g][0][:], ALU.add, [("t1", g), by[g][1]], [("t1", g)])
                yield
                bs = [bank(), bank()]
                for g in range(2):
                    mm(bs[g][0][:], b_tok[:], xd[:, g * 512:(g + 1) * 512], True, True, ["b_tok", "xd"], [bs[g][1]])
                for g in range(2):
                    gp = slice(g * 64, (g + 1) * 64)
                    tt("dve", Sst[l][gp, :].rearrange("p (h d) -> p h d", h=8), Sst[l][gp, :].rearrange("p (h d) -> p h d", h=8),
                       bc(cd[gp, g * 8:(g + 1) * 8], 2, 64), ALU.mult, [("S", l), "cd"], [("S", l)])
                    tt("dve", Sst[l][gp, :], Sst[l][gp, :], bs[g][0][gp, :], ALU.add, [("S", l), bs[g][1]], [("S", l)])
                cp("act", Sbf[l][:], Sst[l][:], [("S", l)], [("Sb", l)])
                yield

            def tail(q):
                qs = slice(q * 128, (q + 1) * 128)
                zq = zsb[q % 2]
                xq = xs_tokb[q % 2]
                zkeys = [("zs", q % 2, 0), ("zs", q % 2, 1)]
                tt("pool", t2[:].rearrange("p (h d) -> p h d", h=16), xq[:].rearrange("p (h d) -> p h d", h=16),
                   bc(BP[:, 32:48], 2, 64), ALU.mult, [("xs_tok", q % 2), ("bp", l)], ["t2"])
                yield
                tt("dve", t1[:], t1[:], t2[:], ALU.add, [("t1", 0), ("t1", 1), "t2"], [("t1", 0), ("t1", 1)])
                yield
                tt("dve", t1[:], t1[:], zq[:], ALU.mult, [("t1", 0), ("t1", 1)] + zkeys, [("t1", 0), ("t1", 1)])
                yield
                act(t2[:], t1[:], AF.Square, [("t1", 0), ("t1", 1)], ["t2"])
                yield
                S.op("dve", lambda e: e.reduce_sum(out=ssq[:], in_=t2[:].rearrange("p (g d) -> p g d", g=2), axis=AX.X),
                     ["t2"], ["ssq"])
                cp("dve", rsv[:], ssq[:], ["ssq"], ["rsv"])
                ts("dve", rsv[:], rsv[:], 1.0 / 512, EPS, ALU.mult, ALU.add, ["rsv"], ["rsv"])
                yield
                act(rsv[:], rsv[:], AF.Ln, ["rsv"], ["rsv"])
                act(rsv[:], rsv[:], AF.Exp, ["rsv"], ["rsv"], scale=-0.5)
                yield
                for g in range(2):
                    hs = slice(g * 512, (g + 1) * 512)
                    ts("dve", vn[:, hs], t1[:, hs], rsv[:, g:g + 1], None, ALU.mult, None, [("t1", g), "rsv"], ["vn"])
                yield
                bk, bkey = bank()
                bkb = bk[:].bitcast(BF16)
                for c in range(8):
                    tr(bkb[:, c * 128:(c + 1) * 128], vn[:, c * 128:(c + 1) * 128], ["vn"], [bkey])
                tt("dve", bufA[:, :, qs], bkb.rearrange("p (c t) -> p c t", c=8), bc(V[:, VP["ng"]:VP["ng"] + 8], 2, 128),
                   ALU.mult, [bkey, ("vp", l)], Akeys)
                yield

            def run(g):
                for _ in g:
                    pass

            def zipper(a, b):
                a_live, b_live = True, True
                while a_live or b_live:
                    if a_live:
                        try:
                            next(a)
                        except StopIteration:
                            a_live = False
                    if b_live:
                        try:
                            next(b)
                        except StopIteration:
                            b_live = False

            def front_mid(q):
                yield from front(q)
                yield from mid(q)

            run(front_mid(0))
            for q in range(1, 4):
                zipper(tail(q - 1), front_mid(q))
            run(tail(3))
            fill(1000)
            S.stage("conf2")
            cp("pool", halo_u[l][:], u_raw[:, :, T:T + 30], [("uraw", i) for i in range(4)], [("hu", l)])
            bm, bmk = bank()
            bq, bqk = bank()
            for i in range(4):
                sq_, sqk = work()
                act(sq_[:], cacc[:, i, :], AF.Square, [("cacc", i)], [sqk])
                mm(bm[:], ones_f[:], cacc[:, i, :], i == 0, i == 3, ["const", ("cacc", i)], [bmk])
                mm(bq[:], ones_f[:], sq_[:], i == 0, i == 3, ["const", sqk], [bqk])
            mean, mk_ = work()
            msq, msk = work()
            rc, rck = work()
            ts("dve", mean[:], bm[:], 1.0 / 512, None, ALU.mult, None, [bmk], [mk_])
            tt("dve", msq[:], mean[:], mean[:], ALU.mult, [mk_], [msk])
            stt("dve", rc[:], bq[:], 1.0 / 512, msq[:], ALU.mult, ALU.subtract, [bqk, msk], [rck])
            rsqrt_inplace(rc[:], rck, 1.0)
            for i in range(4):
                tt("dve", cacc[:, i, :], cacc[:, i, :], mean[:], ALU.subtract, [("cacc", i), mk_], [("cacc", i)])
                tt("dve", cacc[:, i, :], cacc[:, i, :], rc[:], ALU.mult, [("cacc", i), rck], [("cacc", i)])
                act(sbT[:, i, :], cacc[:, i, :], AF.Silu, [("cacc", i), ("vp", l)], [("sbT", i)],
                    bias=V[:, VP["lnb"] + i:VP["lnb"] + i + 1], scale=V[:, VP["lng"] + i:VP["lng"] + i + 1])
            S.stage("sc2")
            ssl, skey = wload([(dgs[l], 1, 12 * 128, 0, 12 * 128)], [("dgs", l)])
            for i in range(4):
                bk, bkey = bank()
                for k in range(3):
                    c_ = (i * 3 + k) * 128
                    mm(bk[:], ssl[:, c_:c_ + 128], p_raw[:, i, k:k + T], k == 0, k == 2,
                       [skey, ("praw", i), "praw_h"], [bkey])
                tt("dve", q_bf[:, i, :], bk[:], gb_sb[:, i, :], ALU.mult, [bkey, ("gb", i)], [("q", i)])
            cp("pool", halo_p[l][:], p_raw[:, :, T:T + 2], [("praw", i) for i in range(4)], [("hp", l)])
            S.stage("merge")
            for qt in range(4):
                c0 = qt * 256
                g1, g1k = wload([(wi[:, OFF_G + c0:OFF_G + c0 + 256], 8, 512, 0, 256),
                                 (wi[:, OFF_G + 1024 + c0:OFF_G + 1024 + c0 + 256], 8, 512, 256, 256)], wbk)
                g2, g2k = wload([(wi[:, OFF_G + 2048 + c0:OFF_G + 2048 + c0 + 256], 8, 256, 0, 256)], wbk)
                g1v, g2v = wview(g1, 8, 512), wview(g2, 8, 256)
                for jj in range(2):
                    j = qt * 2 + jj
                    for wch, (wv_, wk_, co) in enumerate(((g1v, g1k, jj * 128), (g1v, g1k, 256 + jj * 128), (g2v, g2k, jj * 128))):
                        bk, bkey = proj_cm(wv_, wk_, co, 8, hB, hBk)
                        bcol = VP["bg"] + wch * 8 + j
                        act(sig[:, wch * 2 + jj, :], bk[:], AF.Sigmoid, [bkey, ("vp", l)], [("sig", wch * 2 + jj)],
                            bias=V[:, bcol:bcol + 1])
                wa, wak = wload([(wbf["w_ssd_out"][l][:, c0:c0 + 256], 8, 256, 0, 256)], [("wb", "w_ssd_out", l)])
                wb_, wbk2 = wload([(wbf["w_conf_out"][l][:, c0:c0 + 256], 4, 256, 0, 256)], [("wb", "w_conf_out", l)])
                wc_, wck = wload([(wbf["w_sc_out"][l][:, c0:c0 + 256], 4, 256, 0, 256)], [("wb", "w_sc_out", l)])
                wav, wbv, wcv = wview(wa, 8, 256), wview(wb_, 4, 256), wview(wc_, 4, 256)
                for jj in range(2):
                    j = qt * 2 + jj
                    ba, bak = proj_cm(wav, wak, jj * 128, 8, lambda kc: bufA[:, kc, :], lambda kc: [("bufA", kc)])
                    bb, bbk = proj_cm(wbv, wbk2, jj * 128, 4, lambda kc: sbT[:, kc, :], lambda kc: [("sbT", kc)])
                    bc_, bck = proj_cm(wcv, wck, jj * 128, 4, lambda kc: q_bf[:, kc, :], lambda kc: [("q", kc)])
                    m1, m1k = work()
                    m2, m2k = work()
                    m3, m3k = work()
                    tt("dve", m1[:], ba[:], sig[:, 0 + jj, :], ALU.mult, [bak, ("sig", 0 + jj)], [m1k])
                    tt("dve", m2[:], bb[:], sig[:, 2 + jj, :], ALU.mult, [bbk, ("sig", 2 + jj)], [m2k])
                    tt("dve", m3[:], bc_[:], sig[:, 4 + jj, :], ALU.mult, [bck, ("sig", 4 + jj)], [m3k])
                    tt("pool", m1[:], m1[:], m2[:], ALU.add, [m1k, m2k], [m1k])
                    tt("pool", xbcs[:, j, :], m1[:], m3[:], ALU.add, [m1k, m3k], [("xbcs", j)])
            S.stage("wo")
            gate = mod[(l, "mix")]
            for ob in range(2):
                wo, wok = wload([(wbf["w_o"][l][:, ob * 512:(ob + 1) * 512], 8, 512, 0, 512)], [("wb", "w_o", l)])
                wov = wview(wo, 8, 512)
                for jj in range(4):
                    j = ob * 4 + jj
                    bk, bkey = proj_cm(wov, wok, jj * 128, 8, lambda kc: xbcs[:, kc, :], lambda kc: [("xbcs", kc)])
                    stt("dve", x_sb[:, j, :], bk[:], gate[:, 16 + j:17 + j], x_sb[:, j, :], ALU.mult, ALU.add,
                        [bkey, ("mod", l, "mix"), ("x", j)], [("x", j)])

        def ffn(l):
            V = vp[l]
            wu = wbf["w_up"][l]
            S.stage("ffn_h")
            make_h(l, "ffn")
            S.stage("ffn_up")
            gate = mod[(l, "ffn")]
            TL = tails[l]
            abuf = [a_bf, a_bf2]
            for half in range(2):
                AB = abuf[half]
                for pb in range(6):
                    i0 = half * 11 + pb * 2
                    nch = min(2, half * 11 + 11 - i0)
                    ncol = nch * 128
                    slot, wkey = wload([(wu[:, i0 * 128:i0 * 128 + ncol], 8, 512, 0, ncol),
                                        (wu[:, DFF + i0 * 128:DFF + i0 * 128 + ncol], 8, 512, 256, ncol)],
                                       [("wb", "w_up", l)])
                    wv = wview(slot, 8, 512)
                    ch = []
                    for ii in range(nch):
                        i = i0 + ii
                        for (co, ci) in ((ii * 128, i), (256 + ii * 128, 22 + i)):
                            bk, bkey = proj_cm(wv, wkey, co, 8, hB, hBk)
                            acc, akey = work()
                            ch.append((bk, bkey, acc, akey, ci, VP["fw"] + ci * 3))
                    for (bk, bkey, acc, akey, ci, fw) in ch:
                        act(acc[:], bk[:], AF.Identity, [bkey, ("vp", l)], [akey],
                            bias=V[:, VP["fb"] + ci:VP["fb"] + ci + 1], scale=V[:, fw + 2:fw + 3])
                    for (bk, bkey, acc, akey, ci, fw) in ch:
                        stt("dve", acc[:, 1:T], bk[:, 0:T - 1], V[:, fw + 1:fw + 2], acc[:, 1:T], ALU.mult, ALU.add,
                            [bkey, ("vp", l), akey], [akey])
                    for (bk, bkey, acc, akey, ci, fw) in ch:
                        stt("dve", acc[:, 2:T], bk[:, 0:T - 2], V[:, fw:fw + 1], acc[:, 2:T], ALU.mult, ALU.add,
                            [bkey, ("vp", l), akey], [akey])
                    for (bk, bkey, acc, akey, ci, fw) in ch:
                        stt("dve", acc[:, 0:2], TL[:, ci, 0:2], V[:, fw:fw + 1], acc[:, 0:2], ALU.mult, ALU.add,
                            [("tl", l, ci), ("tl", l), ("vp", l), akey], [akey])
                    for (bk, bkey, acc, akey, ci, fw) in ch:
                        stt("dve", acc[:, 0:1], TL[:, ci, 1:2], V[:, fw + 1:fw + 2], acc[:, 0:1], ALU.mult, ALU.add,
                            [("tl", l, ci), ("tl", l), ("vp", l), akey], [akey])
                    for (bk, bkey, acc, akey, ci, fw) in ch:
                        cp("act", TL[:, ci, :], bk[:, T - 2:T], [bkey, ("tl", l)], [("tl", l, ci)])
                    for ii in range(nch):
                        ag, agk = ch[2 * ii][2], ch[2 * ii][3]
                        act(ag[:], ag[:], AF.Silu, [agk], [agk])
                    for ii in range(nch):
                        i = i0 + ii
                        ag, agk = ch[2 * ii][2], ch[2 * ii][3]
                        av, avk = ch[2 * ii + 1][2], ch[2 * ii + 1][3]
                        tt("dve", AB[:, i - half * 11, :], av[:], ag[:], ALU.mult, [agk, avk], [("a_bf", half, i - half * 11)])
            wdn = wbf["w_down"][l]
            for half in range(2):
                AB = abuf[half]
                for jb in range(4):
                    slot, wkey = wload([(wdn[half * 11 * 128:(half + 1) * 11 * 128, jb * 256:(jb + 1) * 256], 11, 256, 0, 256)],
                                       [("wb", "w_down", l)])
                    wv = wview(slot, 11, 256)
                    for jj in range(2):
                        j = jb * 2 + jj
                        bk, bkey = proj_cm(wv, wkey, jj * 128, 11, (lambda AB_: (lambda kc: AB_[:, kc, :]))(AB),
                                           (lambda h_: (lambda kc: [("a_bf", h_, kc)]))(half))
                        stt("dve", x_sb[:, j, :], bk[:], gate[:, 16 + j:17 + j], x_sb[:, j, :], ALU.mult, ALU.add,
                            [bkey, ("mod", l, "ffn"), ("x", j)], [("x", j)])

        xT3 = xT.rearrange("(k p) t -> p k t", p=128)
        oT3 = outT.rearrange("(k p) t -> p k t", p=128)
        chx = [S.chan("x%d" % kc) for kc in range(8)]
        cho = [S.chan("o%d" % i) for i in range(NWK)]

        def load_x(ti):
            t0 = ti * T
            for kc in range(8):
                S.op("sp", (lambda d, s_: (lambda e: e.dma_start(out=d, in_=s_)))(x_sb[:, kc, :], xT3[:, kc, t0:t0 + T]),
                     [], [("x", kc)], dma=chx[kc], force=True)

        load_x(0)
        for ti in range(NT):
            t0 = ti * T
            for l in range(depth):
                mixer(l)
                ffn(l)
            S.stage("final")
            rms(depth - 1)
            gf = VP["gfin"]
            for kc in range(8):
                wt_, wkk = work()
                stt("dve", wt_[:], x_sb[:, kc, :], vp[0][:, gf + kc:gf + kc + 1], rstd[:], ALU.mult, ALU.mult,
                    [("x", kc), ("vp", 0), "rstd"], [wkk])
                S.op("sp", (lambda d, s_: (lambda e: e.dma_start(out=d, in_=s_)))(oT3[:, kc, t0:t0 + T], wt_[:]),
                     [wkk], [("out", ti, kc)], dma=cho[wkk[1]], force=True)
                if ti + 1 < NT:
                    t1_ = (ti + 1) * T
                    S.op("sp", (lambda d, s_: (lambda e: e.dma_start(out=d, in_=s_)))(x_sb[:, kc, :], xT3[:, kc, t1_:t1_ + T]),
                         [], [("x", kc)], dma=chx[kc], force=True)
        S.final_waits("sp")

        with nc.Block() as block:
            @block.tensor
            def _(e):
                S.emit("pe", e)

            @block.scalar
            def _(e):
                S.emit("act", e)

            @block.vector
            def _(e):
                S.emit("dve", e)

            @block.gpsimd
            def _(e):
                S.emit("pool", e)

            @block.sync
            def _(e):
                S.emit("sp", e)
    return nc


def make_packs(inp, depth=DEPTH):
    f = lambda a: np.asarray(a, dtype=np.float32)

    def cm(v):
        v = f(v)
        return v.reshape(-1, 128).T

    def cmk(w):
        w = f(w)
        K, C = w.shape
        return w.T.reshape(C // 128, 128, K).transpose(1, 0, 2).reshape(128, -1)

    vp = np.zeros((depth, 128, NV), np.float32)
    bp = np.zeros((depth, 128, NB), np.float32)
    for l in range(depth):
        def put(name, arr):
            vp[l, :, VP[name]:VP[name] + arr.shape[1]] = arr
        put("g_mix", cm(inp["norm_mix_g"][l]))
        put("g_ffn", cm(inp["norm_ffn_g"][l]))
        put("ab_mix", cm(inp["ada_mix_b"][l]))
        put("ab_ffn", cm(inp["ada_ffn_b"][l]))
        put("xw", cmk(inp["ssd_conv_w"][l]))
        put("xb", cm(inp["ssd_conv_b"][l]))
        put("cw", cmk(inp["conf_conv_w"][l]))
        put("cb", cm(inp["conf_conv_b"][l]))
        put("lng", cm(inp["conf_ln_g"][l]))
        put("lnb", cm(inp["conf_ln_b"][l]))
        put("sw", cmk(inp["sc_conv_w"][l]))
        put("bg", cm(inp["b_gate"][l]))
        put("ng", cm(inp["ssd_norm_g"][l]))
        put("fw", cmk(inp["ffn_conv_w"][l]))
        put("fb", cm(inp["ffn_conv_b"][l]))
        put("gfin", cm(inp["final_norm_g"]))
        bp[l, :, 0:16] = f(inp["ssd_dt_bias"][l])[None, :]
        bp[l, :, 16:32] = f(inp["ssd_a_log"][l])[None, :]
        bp[l, :, 32:48] = f(inp["ssd_d"][l])[None, :]
    return vp, bp


_NC_CACHE = {}


def kernel(**inputs):
    x = np.asarray(inputs["x"], dtype=np.float32)
    B, L, _ = x.shape
    depth = np.asarray(inputs["w_in"]).shape[0]
    key = (L, depth)
    if key not in _NC_CACHE:
        _NC_CACHE[key] = build_nc(L, depth)
    nc = _NC_CACHE[key]
    vp, bp = make_packs(inputs, depth)
    c = np.asarray(inputs["c"], dtype=np.float32)
    wnames = ["ada_mix_w", "ada_ffn_w", "w_in", "w_ssd_out", "w_conf_out", "w_sc_out", "w_o", "w_up", "w_down"]
    wts_ = {n: np.ascontiguousarray(np.asarray(inputs[n], dtype=np.float32)) for n in wnames}
    n_cores = int(os.environ.get("KCORES", "8"))
    in_maps = []
    active = {0: 0, 1: 1, 4: 2, 5: 3} if (n_cores == 8 and B == 4) else {i: (i * B) // n_cores for i in range(n_cores)}
    zero_w = None
    for i in range(n_cores):
        if i in active:
            b = active[i]
            m = {"xT": np.ascontiguousarray(x[b].T), "cpk": np.ascontiguousarray(c[b].reshape(8, 128).T),
                 "vp": vp, "bp": bp}
            m.update(wts_)
        else:
            if zero_w is None:
                zero_w = {n: np.zeros_like(a) for n, a in wts_.items()}
                zero_w.update({"xT": np.zeros((D, L), np.float32), "cpk": np.zeros((128, 8), np.float32),
                               "vp": np.zeros_like(vp), "bp": np.zeros_like(bp)})
            m = dict(zero_w)
        in_maps.append(m)
    res = run_bass_kernel_spmd(nc, in_maps, core_ids=list(range(n_cores)))
    out = np.empty((B, L, D), np.float32)
    inv = {}
    for i, b in active.items():
        inv.setdefault(b, i)
    for b in range(B):
        i = inv.get(b, 0)
        out[b] = np.asarray(res.results[i]["outT"]).T
    return out
```

```python
import os
import numpy as np
from contextlib import ExitStack
import concourse.bass as bass
import concourse.mybir as mybir
from concourse.bass_utils import run_bass_kernel_spmd

F32 = mybir.dt.float32
BF16 = mybir.dt.bfloat16
AF = mybir.ActivationFunctionType
ALU = mybir.AluOpType
AX = mybir.AxisListType

D = 1024
DEPTH = 2
NIN = 7952
DFF = 2816
EPS = 1e-6
T = 512
OFF_XBC = 1024
OFF_DT = 2304
OFF_CONF = 2320
OFF_SC = 3344
OFF_G = 4880

VP = {}
_o = 0
for _n, _w in [("g_mix", 8), ("g_ffn", 8), ("ab_mix", 24), ("ab_ffn", 24), ("xw", 40), ("xb", 10),
               ("cw", 124), ("cb", 4), ("lng", 4), ("lnb", 4), ("sw", 12), ("bg", 24), ("ng", 8),
               ("fw", 132), ("fb", 44), ("gfin", 8)]:
    VP[_n] = _o
    _o += _w
NV = _o
NB = 16 + 16 + 16


def bc(ap, pos, n):
    l = [list(x) for x in ap.ap]
    l.insert(pos, [0, n])
    return bass.AP(ap.tensor, ap.offset, l)


def bclast(ap, n):
    l = [list(x) for x in ap.ap]
    assert l[-1][1] == 1
    l[-1] = [0, n]
    return bass.AP(ap.tensor, ap.offset, l)


class Sched:
    ENG = ("pe", "act", "dve", "pool", "sp")

    def __init__(self, nc, es):
        self.nc = nc
        self.es = es
        self.ops = {n: [] for n in self.ENG}
        self.count = {n: 0 for n in self.ENG}
        self.seen = {n: {} for n in self.ENG}
        self.sems = {n: es.enter_context(nc.semaphore("s_" + n)) for n in self.ENG}
        self.res = {}
        self.chans = []
        self.nbank = 0

    def chan(self, name):
        sem = self.es.enter_context(self.nc.semaphore("c_" + name))
        self.chans.append({"sem": sem, "count": 0})
        return len(self.chans) - 1

    def _r(self, k):
        r = self.res.get(k)
        if r is None:
            r = {"w": None, "rd": {}}
            self.res[k] = r
        return r

    def stage(self, name):
        self.nstage = getattr(self, "nstage", 0) + 1
        lim = int(os.environ.get("KSTAGE", "0"))
        if lim and self.nstage > lim and not getattr(self, "muted", False):
            self.muted = True
            print("MUTED before stage", self.nstage, name)

    def op(self, eng, fn, reads=(), writes=(), dma=None, force=False):
        if getattr(self, "muted", False) and not force:
            return
        waits = {}
        seen = self.seen[eng]

        def need(ev):
            if ev is None:
                return
            kind, who, val = ev
            if kind == "e" and who == eng and eng == "pe":
                return
            key = (kind, who)
            if seen.get(key, 0) >= val:
                return
            if waits.get(key, 0) < val:
                waits[key] = val

        for k in reads:
            r = self._r(k)
            need(r["w"])
            if isinstance(k, tuple) and k[0] == "ps":
                for ev in r["rd"].values():
                    if ev[1] != eng:
                        need(ev)
        for k in writes:
            r = self._r(k)
            need(r["w"])
            for ev in r["rd"].values():
                need(ev)
        for key, val in waits.items():
            seen[key] = val
        if dma is None:
            self.count[eng] += 1
            ev = ("e", eng, self.count[eng])
        else:
            c = self.chans[dma]
            c["count"] += 16
            ev = ("d", dma, c["count"])
        for k in reads:
            self._r(k)["rd"][(ev[0], ev[1])] = ev
        for k in writes:
            r = self._r(k)
            r["w"] = ev
            r["rd"] = {}
        self.ops[eng].append((waits, fn, ev))

    def final_waits(self, eng):
        waits = {}
        for i, c in enumerate(self.chans):
            if c["count"] > 0:
                waits[("d", i)] = c["count"]
        self.ops[eng].append((waits, None, None))

    def emit(self, name, e):
        for waits, fn, ev in self.ops[name]:
            for (kind, who), val in waits.items():
                sem = self.sems[who] if kind == "e" else self.chans[who]["sem"]
                e.wait_ge(sem, val)
            if fn is None:
                continue
            ins = fn(e)
            if ev[0] == "e":
                ins.then_inc(self.sems[name], 1)
            else:
                ins.then_inc(self.chans[ev[1]]["sem"], 16)


def build_nc(L, depth=DEPTH):
    NT = L // T
    nc = bass.Bass("TRN2", target_bir_lowering=False)
    dt_ = nc.dram_tensor
    xT = dt_("xT", [D, L], F32, kind="ExternalInput").ap()
    cpk = dt_("cpk", [128, 8], F32, kind="ExternalInput").ap()
    vp_d = dt_("vp", [depth, 128, NV], F32, kind="ExternalInput").ap()
    bp_d = dt_("bp", [depth, 128, NB], F32, kind="ExternalInput").ap()
    wsrc = {}
    wshape = {"ada_mix_w": [D, 3 * D], "ada_ffn_w": [D, 3 * D], "w_in": [D, NIN], "w_ssd_out": [D, D],
              "w_conf_out": [512, D], "w_sc_out": [512, D], "w_o": [D, D], "w_up": [D, 2 * DFF],
              "w_down": [DFF, D]}
    wbf = {}
    for n, shp in wshape.items():
        wsrc[n] = dt_(n, [depth] + shp, F32, kind="ExternalInput").ap()
        wbf[n] = dt_(n + "_bf", [depth] + shp, BF16, kind="Internal").ap()
    outT = dt_("outT", [D, L], F32, kind="ExternalOutput").ap()
    dgw = dt_("dgw", [depth, 4, 128, 31 * 128], BF16, kind="Internal").ap()
    dgx = dt_("dgx", [depth, 2, 128, 20 * 128], BF16, kind="Internal").ap()
    dgs = dt_("dgs", [depth, 128, 12 * 128], BF16, kind="Internal").ap()

    es = ExitStack()
    with es:
        S = Sched(nc, es)

        def sb(name, shape, dtype):
            return es.enter_context(nc.sbuf_tensor(name, shape, dtype))

        x_sb = sb("x_sb", [128, 8, T], F32)
        bufA = sb("bufA", [128, 8, T], BF16)
        bufB = sb("bufB", [128, 8, T], BF16)
        rstd = sb("rstd", [128, T], F32)
        NWK = 5
        wk = [sb(f"wk{i}", [128, T], F32) for i in range(NWK)]
        NW = 5
        wslot = [sb(f"ws{i}", [128, 4096], BF16) for i in range(NW)]
        wchan = [S.chan(f"w{i}") for i in range(NW)]
        Sst = [sb(f"S{l}", [128, 512], F32) for l in range(depth)]
        Sbf = [sb(f"Sb{l}", [128, 512], BF16) for l in range(depth)]
        xbc_raw = sb("xbc_raw", [128, 10, 3 + T], BF16)
        xbcs = sb("xbcs", [128, 10, T], BF16)
        u_raw = sb("u_raw", [128, 4, 30 + T], BF16)
        cacc = sb("cacc", [128, 4, T], F32)
        sbT = sb("sbT", [128, 4, T], BF16)
        p_raw = sb("p_raw", [128, 4, 2 + T], BF16)
        gb_sb = sb("gb_sb", [128, 4, T], BF16)
        q_bf = sb("q_bf", [128, 4, T], BF16)
        sig = sb("sig", [128, 6, T], BF16)
        a_bf = sb("a_bf", [128, 11, T], BF16)
        a_bf2 = sb("a_bf2", [128, 11, T], BF16)
        halo_x = [sb(f"hx{l}", [128, 10, 3], BF16) for l in range(depth)]
        halo_u = [sb(f"hu{l}", [128, 4, 30], BF16) for l in range(depth)]
        halo_p = [sb(f"hp{l}", [128, 4, 2], BF16) for l in range(depth)]
        tails = [sb(f"tl{l}", [128, 44, 2], F32) for l in range(depth)]
        zsb = [sb(f"zs{i}", [128, 1024], BF16) for i in range(2)]
        decT = [sb(f"decT{i}", [128, 4, 128], BF16) for i in range(2)]
        wts = sb("wts", [128, 16, 128], BF16)
        cbt = sb("cbt", [128, 2, 128], BF16)
        xs_tokb = [sb(f"xs_tok{i}", [128, 1024], BF16) for i in range(2)]
        xc = sb("xc", [128, 1024], BF16)
        xd = sb("xd", [128, 1024], BF16)
        b_tok = sb("b_tok", [128, 128], BF16)
        t1 = sb("t1", [128, 1024], F32)
        t2 = sb("t2", [128, 1024], F32)
        vn = sb("vn", [128, 1024], BF16)
        dtx = sb("dtx", [128, 64], F32)
        dta = sb("dta", [128, 64], F32)
        dte = sb("dte", [128, 64], F32)
        dtp = sb("dtp", [128, 64], F32)
        adt = sb("adt", [128, 64], F32)
        ahl = sb("ahl", [128, 128], BF16)
        nacum = sb("nacum", [128, 64], F32)
        eacum = sb("eacum", [128, 64], F32)
        acl = sb("acl", [128, 16], F32)
        dsx = sb("dsx", [128, 16], F32)
        ds = sb("ds", [128, 16], F32)
        cd = sb("cd", [128, 16], F32)
        ssq = sb("ssq", [128, 2], F32)
        rsv = sb("rsv", [128, 2], F32)
        ident = sb("ident", [128, 128], BF16)
        tri = sb("tri", [128, 128], BF16)
        mneg = sb("mneg", [128, 128], BF16)
        ones_bf = sb("ones_bf", [128, 128], BF16)
        ones_f = sb("ones_f", [128, 128], F32)
        wconst = sb("wconst", [128, 512], BF16)
        vp = [sb(f"vp{l}", [128, NV], F32) for l in range(depth)]
        bp = [sb(f"bp{l}", [128, NB], F32) for l in range(depth)]
        arep = [sb(f"arep{l}", [128, 16], F32) for l in range(depth)]
        mod = {(l, w): sb(f"mod{l}{w}", [128, 24], F32) for l in range(depth) for w in ("mix", "ffn")}
        gsv = {(l, w): sb(f"gs{l}{w}", [128, 8], F32) for l in range(depth) for w in ("mix", "ffn")}
        c_sb = sb("c_sb", [128, 8], F32)
        c_sig = sb("c_sig", [128, 8], F32)
        c_bf = sb("c_bf", [128, 8], BF16)
        banks = [es.enter_context(nc.psum_tensor(f"pb{i}", [128, 512], F32)) for i in range(8)]

        ch_misc = S.chan("misc")
        ch_x = S.chan("x")
        ch_out = S.chan("out")

        def bank():
            i = S.nbank % 7
            S.nbank += 1
            return banks[i], ("ps", i)

        def warm(n):
            for _ in range(n):
                mm(banks[7][:], ones_bf[:], wconst[:], True, True, ["const"], [("ps", 7)])

        nwk = [0]

        def work():
            i = nwk[0] % NWK
            nwk[0] += 1
            return wk[i], ("wk", i)

        nws = [0]

        def wload(pieces, key_reads):
            i = nws[0] % NW
            nws[0] += 1
            slot = wslot[i]
            for (src, nkc, width, coff, ncols) in pieces:
                dst = slot[:, 0:nkc * width].rearrange("p (k c) -> p k c", k=nkc)[:, :, coff:coff + ncols]
                s3 = src.rearrange("(k p) c -> p k c", p=128)
                S.op("sp", (lambda d, s: (lambda e: e.dma_start(out=d, in_=s)))(dst, s3),
                     reads=key_reads, writes=[("ws", i)], dma=wchan[i])
            return slot, ("ws", i)

        def wview(slot, nkc, width):
            return slot[:, 0:nkc * width].rearrange("p (k c) -> p k c", k=nkc)

        def mm(out, lhsT, rhs, start, stop, reads, writes):
            S.op("pe", lambda e: e.matmul(out, lhsT, rhs, start=start, stop=stop), reads, writes)

        def tr(out, in_, reads, writes):
            S.op("pe", lambda e: e.transpose(out, in_, ident[:]), reads + ["const"], writes)

        def act(out, in_, func, reads, writes, bias=None, scale=None):
            kw = {}
            if bias is not None:
                kw["bias"] = bias
            if scale is not None:
                kw["scale"] = scale
            S.op("act", lambda e: e.activation(out=out, in_=in_, func=func, **kw), reads, writes)

        def tt(eng, out, in0, in1, op, reads, writes):
            S.op(eng, lambda e: e.tensor_tensor(out=out, in0=in0, in1=in1, op=op), reads, writes)

        def ts(eng, out, in0, s1, s2, op0, op1, reads, writes):
            if op1 is None:
                S.op(eng, lambda e: e.tensor_scalar(out=out, in0=in0, scalar1=s1, scalar2=None, op0=op0), reads, writes)
            else:
                S.op(eng, lambda e: e.tensor_scalar(out=out, in0=in0, scalar1=s1, scalar2=s2, op0=op0, op1=op1), reads, writes)

        def stt(eng, out, in0, scalar, in1, op0, op1, reads, writes):
            S.op(eng, lambda e: e.scalar_tensor_tensor(out=out, in0=in0, scalar=scalar, in1=in1, op0=op0, op1=op1),
                 reads, writes)

        def cp(eng, out, in_, reads, writes):
            if eng == "act":
                S.op(eng, lambda e: e.activation(out=out, in_=in_, func=AF.Copy), reads, writes)
            else:
                S.op(eng, lambda e: e.tensor_copy(out=out, in_=in_), reads, writes)

        def rsqrt_inplace(ap, key, scale):
            ts("dve", ap, ap, scale, EPS, ALU.mult, ALU.add, [key], [key])
            act(ap, ap, AF.Ln, [key], [key])
            act(ap, ap, AF.Exp, [key], [key], scale=-0.5)

        cast_order = []
        for l in range(depth):
            cast_order += [("ada_mix_w", l), ("ada_ffn_w", l)]
        for l in range(depth):
            cast_order += [("w_in", l), ("w_ssd_out", l), ("w_conf_out", l), ("w_sc_out", l), ("w_o", l),
                           ("w_up", l), ("w_down", l)]
        for (n, l) in cast_order:
            rows = wshape[n][0]
            step = 256
            ch_cast = S.chan("cast_%s%d" % (n, l))
            for r0 in range(0, rows, step):
                r1 = min(rows, r0 + step)
                S.op("pool", (lambda d, s: (lambda e: e.dma_start(out=d, in_=s)))(wbf[n][l, r0:r1, :], wsrc[n][l, r0:r1, :]),
                     reads=[], writes=[("wb", n, l)], dma=ch_cast)

        S.op("pool", lambda e: e.memset(ident[:], 1.0), [], ["const"])
        S.op("pool", lambda e: e.affine_select(out=ident[:], in_=ident[:], pattern=[[1, 128]], compare_op=ALU.is_equal,
                                               fill=0.0, base=0, channel_multiplier=-1), [], ["const"])
        S.op("pool", lambda e: e.memset(tri[:], 1.0), [], ["const"])
        S.op("pool", lambda e: e.affine_select(out=tri[:], in_=tri[:], pattern=[[1, 128]], compare_op=ALU.is_ge,
                                               fill=0.0, base=0, channel_multiplier=-1), [], ["const"])
        S.op("pool", lambda e: e.memset(mneg[:], 0.0), [], ["const"])
        S.op("pool", lambda e: e.affine_select(out=mneg[:], in_=mneg[:], pattern=[[1, 128]], compare_op=ALU.is_ge,
                                               fill=-30000.0, base=0, channel_multiplier=-1), [], ["const"])
        S.op("pool", lambda e: e.memset(ones_bf[:], 1.0), [], ["const"])
        S.op("pool", lambda e: e.memset(ones_f[:], 1.0), [], ["const"])
        S.op("pool", lambda e: e.memset(wconst[:], 0.5), [], ["const"])
        for l in range(depth):
            S.op("pool", (lambda a: (lambda e: e.memset(a, 0.0)))(Sst[l][:]), [], [("S", l)])
            S.op("pool", (lambda a: (lambda e: e.memset(a, 0.0)))(Sbf[l][:]), [], [("Sb", l)])
            S.op("pool", (lambda a: (lambda e: e.memset(a, 0.0)))(halo_x[l][:]), [], [("hx", l)])
            S.op("pool", (lambda a: (lambda e: e.memset(a, 0.0)))(halo_u[l][:]), [], [("hu", l)])
            S.op("pool", (lambda a: (lambda e: e.memset(a, 0.0)))(halo_p[l][:]), [], [("hp", l)])
            S.op("pool", (lambda a: (lambda e: e.memset(a, 0.0)))(tails[l][:]), [], [("tl", l)])
            S.op("sp", (lambda d, s: (lambda e: e.dma_start(out=d, in_=s)))(vp[l][:], vp_d[l]), [], [("vp", l)], dma=S.chan("vp%d" % l))
            S.op("sp", (lambda d, s: (lambda e: e.dma_start(out=d, in_=s)))(bp[l][:], bp_d[l]), [], [("bp", l)], dma=S.chan("bp%d" % l))
        S.op("sp", lambda e: e.dma_start(out=c_sb[:], in_=cpk), [], ["c"], dma=ch_misc)
        act(c_sig[:], c_sb[:], AF.Sigmoid, ["c"], ["csig"])
        tt("dve", c_bf[:], c_sb[:], c_sig[:], ALU.mult, ["c", "csig"], ["cbf"])
        for l in range(depth):
            act(arep[l][:], bp[l][:, 16:32], AF.Exp, [("bp", l)], [("arep", l)])
            ts("dve", arep[l][:], arep[l][:], -1.0, None, ALU.mult, None, [("arep", l)], [("arep", l)])
            for w in ("mix", "ffn"):
                wn = "ada_%s_w" % w
                bk, bkey = bank()
                for blk in range(6):
                    slot, wkey = wload([(wbf[wn][l, :, blk * 512:(blk + 1) * 512], 8, 512, 0, 512)], [("wb", wn, l)])
                    wv = wview(slot, 8, 512)
                    for oc4 in range(4):
                        oc = blk * 4 + oc4
                        for kc in range(8):
                            mm(bk[:, oc:oc + 1], wv[:, kc, oc4 * 128:(oc4 + 1) * 128], c_bf[:, kc:kc + 1],
                               kc == 0, kc == 7, [wkey, "cbf"], [bkey])
                ab = VP["ab_" + w]
                tt("dve", mod[(l, w)][:], bk[:, 0:24], vp[l][:, ab:ab + 24], ALU.add, [bkey, ("vp", l)], [("mod", l, w)])
                g0 = VP["g_" + w]
                stt("dve", gsv[(l, w)][:], mod[(l, w)][:, 8:16], 1.0, vp[l][:, g0:g0 + 8], ALU.add, ALU.mult,
                    [("mod", l, w), ("vp", l)], [("gs", l, w)])

        for l in range(depth):
            for i in range(4):
                si = nws[0] % NW
                nws[0] += 1
                cw = VP["cw"] + i * 31
                for k in range(31):
                    ts("dve", wslot[si][:, k * 128:(k + 1) * 128], ident[:], vp[l][:, cw + k:cw + k + 1], None, ALU.mult, None,
                       ["const", ("vp", l)], [("ws", si)])
                S.op("sp", (lambda d, s_: (lambda e: e.dma_start(out=d, in_=s_)))(dgw[l, i], wslot[si][:, 0:31 * 128]),
                     [("ws", si)], [("dgw", l, i)], dma=S.chan("dgw%d_%d" % (l, i)))
        for l in range(depth):
            for gi in range(2):
                si = nws[0] % NW
                nws[0] += 1
                for cj in range(5):
                    xw = VP["xw"] + (gi * 5 + cj) * 4
                    for k in range(4):
                        c_ = (cj * 4 + k) * 128
                        ts("dve", wslot[si][:, c_:c_ + 128], ident[:], vp[l][:, xw + k:xw + k + 1], None, ALU.mult, None,
                           ["const", ("vp", l)], [("ws", si)])
                S.op("sp", (lambda d, s_: (lambda e: e.dma_start(out=d, in_=s_)))(dgx[l, gi], wslot[si][:, 0:20 * 128]),
                     [("ws", si)], [("dgx", l, gi)], dma=S.chan("dgx%d_%d" % (l, gi)))
            si = nws[0] % NW
            nws[0] += 1
            for i in range(4):
                sw = VP["sw"] + i * 3
                for k in range(3):
                    c_ = (i * 3 + k) * 128
                    ts("dve", wslot[si][:, c_:c_ + 128], ident[:], vp[l][:, sw + k:sw + k + 1], None, ALU.mult, None,
                       ["const", ("vp", l)], [("ws", si)])
            S.op("sp", (lambda d, s_: (lambda e: e.dma_start(out=d, in_=s_)))(dgs[l], wslot[si][:, 0:12 * 128]),
                 [("ws", si)], [("dgs", l)], dma=S.chan("dgs%d" % l))
        xkeys = [("x", j) for j in range(8)]
        Akeys = [("bufA", j) for j in range(8)]
        Bkeys = [("bufB", j) for j in range(8)]

        def rms(l):
            bk, bkey = bank()
            for kc in range(8):
                act(bufA[:, kc, :], x_sb[:, kc, :], AF.Square, [("x", kc)], [("bufA", kc)])
                mm(bk[:], ones_bf[:], bufA[:, kc, :], kc == 0, kc == 7, [("bufA", kc), "const"], [bkey])
            ts("dve", rstd[:], bk[:], 1.0 / D, EPS, ALU.mult, ALU.add, [bkey], ["rstd"])
            act(rstd[:], rstd[:], AF.Ln, ["rstd"], ["rstd"])
            act(rstd[:], rstd[:], AF.Exp, ["rstd"], ["rstd"], scale=-0.5)

        def make_h(l, w):
            rms(l)
            warm(48)
            sh = mod[(l, w)]
            for kc in range(8):
                wt_, wkk = work()
                stt("dve", wt_[:], x_sb[:, kc, :], gsv[(l, w)][:, kc:kc + 1], rstd[:], ALU.mult, ALU.mult,
                    [("x", kc), ("gs", l, w), "rstd"], [wkk])
                act(bufB[:, kc, :], wt_[:], AF.Identity, [wkk, ("mod", l, w)], [("bufB", kc)], bias=sh[:, kc:kc + 1])

        def proj_cm(wv, wkey, c0, nk, rhs_fn, rhs_keys):
            bk, bkey = bank()
            for kc in range(nk):
                mm(bk[:], wv[:, kc, c0:c0 + 128], rhs_fn(kc), kc == 0, kc == nk - 1, [wkey] + rhs_keys(kc), [bkey])
            return bk, bkey

        hB = (lambda kc: bufB[:, kc, :])
        hBk = (lambda kc: [("bufB", kc)])

        def mixer(l):
            V = vp[l]
            wi = wbf["w_in"][l]
            wbk = [("wb", "w_in", l)]
            S.stage("mix_h")
            make_h(l, "mix")
            S.stage("xbc")
            wd, wdk = wload([(wi[:, OFF_DT:OFF_DT + 16], 8, 16, 0, 16)], wbk)
            wdv = wview(wd, 8, 16)
            BP = bp[l]
            bkd, bkdk = bank()
            for q in range(4):
                qs = slice(q * 128, (q + 1) * 128)
                for kc in range(8):
                    mm(bkd[:, q * 16:(q + 1) * 16], bufB[:, kc, qs], wdv[:, kc, :], kc == 0, kc == 7, [wdk, ("bufB", kc)], [bkdk])
            q4 = lambda ap: ap.rearrange("p (q h) -> p q h", q=4)
            tt("dve", q4(dtx[:]), q4(bkd[:, 0:64]), bc(BP[:, 0:16], 1, 4), ALU.add, [bkdk, ("bp", l)], ["dtx"])
            act(dta[:], dtx[:], AF.Abs, ["dtx"], ["dta"])
            act(dte[:], dta[:], AF.Exp, ["dta"], ["dte"], scale=-1.0)
            act(dte[:], dte[:], AF.Ln, ["dte"], ["dte"], bias=1.0)
            stt("dve", dtp[:], dtx[:], 0.0, dte[:], ALU.max, ALU.add, ["dtx", "dte"], ["dtp"])
            tt("dve", q4(adt[:]), q4(dtp[:]), bc(arep[l][:], 1, 4), ALU.mult, ["dtp", ("arep", l)], ["adt"])
            cp("dve", ahl[:, 0:64], adt[:], ["adt"], ["ahl"])
            tt("dve", ahl[:, 64:128], adt[:], ahl[:, 0:64], ALU.subtract, ["adt", "ahl"], ["ahl"])
            cp("pool", xbc_raw[:, :, 0:3], halo_x[l][:], [("hx", l)], ["xraw_h"])
            for (b0, ncol) in ((0, 512), (512, 512), (1024, 256)):
                slot, wkey = wload([(wi[:, OFF_XBC + b0:OFF_XBC + b0 + ncol], 8, 512, 0, ncol)], wbk)
                wv = wview(slot, 8, 512)
                for i in range(ncol // 128):
                    ci = b0 // 128 + i
                    bk, bkey = proj_cm(wv, wkey, i * 128, 8, hB, hBk)
                    cp("act", xbc_raw[:, ci, 3:3 + T], bk[:], [bkey], [("xraw", ci)])
            S.stage("conf1")
            cp("pool", u_raw[:, :, 0:30], halo_u[l][:], [("hu", l)], ["uraw_h"])
            sv, svk = wload([(wi[:, OFF_CONF:OFF_CONF + 512], 8, 512, 0, 512)], wbk)
            sg_, sgk = wload([(wi[:, OFF_CONF + 512:OFF_CONF + 1024], 8, 512, 0, 512)], wbk)
            svv, sgv = wview(sv, 8, 512), wview(sg_, 8, 512)
            for i in range(4):
                bv, bvk = proj_cm(svv, svk, i * 128, 8, hB, hBk)
                bg, bgk = proj_cm(sgv, sgk, i * 128, 8, hB, hBk)
                sgm, sgmk = work()
                act(sgm[:], bg[:], AF.Sigmoid, [bgk], [sgmk])
                tt("dve", u_raw[:, i, 30:30 + T], bv[:], sgm[:], ALU.mult, [bvk, sgmk], [("uraw", i)])
            S.stage("sc1")
            cp("pool", p_raw[:, :, 0:2], halo_p[l][:], [("hp", l)], ["praw_h"])
            wsl = [wload([(wi[:, OFF_SC + j * 512:OFF_SC + (j + 1) * 512], 8, 512, 0, 512)], wbk) for j in range(3)]
            wvv = [wview(s_, 8, 512) for (s_, _) in wsl]
            for i in range(4):
                bgb, bgbk = proj_cm(wvv[0], wsl[0][1], i * 128, 8, hB, hBk)
                cp("act", gb_sb[:, i, :], bgb[:], [bgbk], [("gb", i)])
                bgc, bgck = proj_cm(wvv[1], wsl[1][1], i * 128, 8, hB, hBk)
                gcs, gck = work()
                cp("act", gcs[:], bgc[:], [bgck], [gck])
                bxv, bxvk = proj_cm(wvv[2], wsl[2][1], i * 128, 8, hB, hBk)
                tt("dve", p_raw[:, i, 2:2 + T], bxv[:], gcs[:], ALU.mult, [bxvk, gck], [("praw", i)])
            for gi in range(2):
                xsl, xkey = wload([(dgx[l, gi], 1, 20 * 128, 0, 20 * 128)], [("dgx", l, gi)])
                for cj in range(5):
                    ci = gi * 5 + cj
                    bk, bkey = bank()
                    for k in range(4):
                        c_ = (cj * 4 + k) * 128
                        mm(bk[:], xsl[:, c_:c_ + 128], xbc_raw[:, ci, k:k + T], k == 0, k == 3,
                           [xkey, ("xraw", ci), "xraw_h"], [bkey])
                    act(xbcs[:, ci, :], bk[:], AF.Silu, [bkey, ("vp", l)], [("xbcs", ci)],
                        bias=V[:, VP["xb"] + ci:VP["xb"] + ci + 1])
            cp("pool", halo_x[l][:], xbc_raw[:, :, T:T + 3], [("xraw", ci) for ci in range(10)], [("hx", l)])
            def conf_conv_pe(i):
                dsl, dkey = wload([(dgw[l, i], 1, 31 * 128, 0, 31 * 128)], [("dgw", l, i)])
                bk, bkey = bank()
                for k in range(31):
                    mm(bk[:], dsl[:, k * 128:(k + 1) * 128], u_raw[:, i, k:k + T], k == 0, k == 30,
                       [dkey, ("uraw", i), "uraw_h"], [bkey])
                act(cacc[:, i, :], bk[:], AF.Identity, [bkey, ("vp", l)], [("cacc", i)],
                    bias=V[:, VP["cb"] + i:VP["cb"] + i + 1])

            pend = []

            def fill(n=1):
                for _ in range(n):
                    if not pend:
                        return
                    k, i = pend.pop(0)
                    cw = VP["cw"] + i * 31
                    stt("dve", cacc[:, i, :], u_raw[:, i, k:k + T], V[:, cw + k:cw + k + 1], cacc[:, i, :], ALU.mult, ALU.add,
                        [("uraw", i), "uraw_h", ("vp", l), ("cacc", i)], [("cacc", i)])
            S.stage("ssd")
            z0, zk0 = wload([(wi[:, 0:512], 8, 512, 0, 512)], wbk)
            z1, zk1 = wload([(wi[:, 512:1024], 8, 512, 0, 512)], wbk)
            zv = [wview(z0, 8, 512), wview(z1, 8, 512)]
            zk = [zk0, zk1]
            bka, bkak = bank()
            for q in range(4):
                mm(bka[:, q * 16:(q + 1) * 16], tri[:], ahl[:, q * 16:(q + 1) * 16], True, False, ["const", "ahl"], [bkak])
                mm(bka[:, q * 16:(q + 1) * 16], tri[:], ahl[:, 64 + q * 16:64 + (q + 1) * 16], False, True, ["const", "ahl"], [bkak])
            act(nacum[:], bka[:, 0:64], AF.Copy, [bkak], ["nacum"], scale=-1.0)
            act(eacum[:], bka[:, 0:64], AF.Exp, [bkak], ["eacum"])
            def front(q):
                qs = slice(q * 128, (q + 1) * 128)
                zq = zsb[q % 2]
                xq = xs_tokb[q % 2]
                for zb in range(2):
                    bk, bkey = bank()
                    for kc in range(8):
                        mm(bk[:], bufB[:, kc, qs], zv[zb][:, kc, :], kc == 0, kc == 7, [zk[zb], ("bufB", kc)], [bkey])
                    act(zq[:, zb * 512:(zb + 1) * 512], bk[:], AF.Silu, [bkey], [("zs", q % 2, zb)])
                    yield
                conf_conv_pe(q)
                yield
                for g in range(2):
                    gp = slice(g * 64, (g + 1) * 64)
                    bk, bkey = bank()
                    mm(bk[:, 0:128], xbcs[gp, 8, qs], xbcs[gp, 9, qs], True, True, [("xbcs", 8), ("xbcs", 9)], [bkey])
                    cp("act", cbt[:, g, :], bk[:, 0:128], [bkey], ["cbt"])
                yield
                for hq in range(4):
                    bk, bkey = bank()
                    for i in range(4):
                        h = hq * 4 + i
                        o = bk[:, i * 128:(i + 1) * 128]
                        mm(o, bclast(ahl[:, q * 16 + h:q * 16 + h + 1], 128), tri[:], True, False, ["ahl", "const"], [bkey])
                        mm(o, bclast(ahl[:, 64 + q * 16 + h:64 + q * 16 + h + 1], 128), tri[:], False, False, ["ahl", "const"], [bkey])
                        mm(o, ident[:], mneg[:], False, True, ["const"], [bkey])
                    dT = decT[hq % 2]
                    dk = ("decT", hq % 2)
                    for i in range(4):
                        h = hq * 4 + i
                        act(dT[:, i, :], bk[:, i * 128:(i + 1) * 128], AF.Exp, [bkey, "nacum"], [dk],
                            bias=nacum[:, q * 16 + h:q * 16 + h + 1])
                    cp("dve", acl[:, hq * 4:(hq + 1) * 4], bk[:, 127::128], [bkey], ["acl"])
                    tt("dve", wts[:, hq * 4:(hq + 1) * 4, :], dT[:], bc(cbt[:, hq // 2, :], 1, 4), ALU.mult,
                       [dk, "cbt"], [("wts", hq)])
                    yield
                tt("dve", dsx[:], acl[:], nacum[:, q * 16:(q + 1) * 16], ALU.add, ["acl", "nacum"], ["dsx"])
                act(ds[:], dsx[:], AF.Exp, ["dsx"], ["ds"])
                act(cd[:], acl[:], AF.Exp, ["acl"], ["cd"])
                yield
                bk, bkey = bank()
                bkb = bk[:].bitcast(BF16)
                for c in range(8):
                    tr(bkb[:, c * 128:(c + 1) * 128], xbcs[:, c, qs], [("xbcs", c)], [bkey])
                cp("act", xq[:], bkb, [bkey], [("xs_tok", q % 2)])
                tt("dve", xc[:].rearrange("p (h d) -> p h d", h=16), bkb.rearrange("p (h d) -> p h d", h=16),
                   bc(dtp[:, q * 16:(q + 1) * 16], 2, 64), ALU.mult, [bkey, "dtp"], ["xc"])
                yield
                tt("pool", xd[:].rearrange("p (h d) -> p h d", h=16), xc[:].rearrange("p (h d) -> p h d", h=16),
                   bc(ds[:], 2, 64), ALU.mult, ["xc", "ds"], ["xd"])
                bk, bkey = bank()
                bkb2 = bk[:].bitcast(BF16)
                tr(bkb2[:, 0:128], xbcs[:, 8, qs], [("xbcs", 8)], [bkey])
                cp("act", b_tok[:], bkb2[:, 0:128], [bkey], ["b_tok"])
                yield

            def mid(q):
                qs = slice(q * 128, (q + 1) * 128)
                by = [bank(), bank()]
                for h in range(16):
                    b_, bk_ = by[h // 8]
                    mm(b_[:, (h % 8) * 64:(h % 8 + 1) * 64], wts[:, h, :], xc[:, h * 64:(h + 1) * 64], True, True,
                       [("wts", h // 4), "xc"], [bk_])
                bo = [bank(), bank()]
                for g in range(2):
                    gp = slice(g * 64, (g + 1) * 64)
                    mm(bo[g][0][:], xbcs[gp, 9, qs], Sbf[l][gp, :], True, True, [("xbcs", 9), ("Sb", l)], [bo[g][1]])
                for g in range(2):
                    hs = slice(g * 512, (g + 1) * 512)
                    tt("dve", t1[:, hs].rearrange("p (h d) -> p h d", h=8), bo[g][0][:].rearrange("p (h d) -> p h d", h=8),
                       bc(eacum[:, q * 16 + g * 8:q * 16 + (g + 1) * 8], 2, 64), ALU.mult, [bo[g][1], "eacum"], [("t1", g)])
                    tt("dve", t1[:, hs], t1[:, hs], by[g][0][:], ALU.add, [("t1", g), by[g][1]], [("t1", g)])
                yield
                bs = [bank(), bank()]
                for g in range(2):
                    mm(bs[g][0][:], b_tok[:], xd[:, g * 512:(g + 1) * 512], True, True, ["b_tok", "xd"], [bs[g][1]])
                for g in range(2):
                    gp = slice(g * 64, (g + 1) * 64)
                    tt("dve", Sst[l][gp, :].rearrange("p (h d) -> p h d", h=8), Sst[l][gp, :].rearrange("p (h d) -> p h d", h=8),
                       bc(cd[gp, g * 8:(g + 1) * 8], 2, 64), ALU.mult, [("S", l), "cd"], [("S", l)])
                    tt("dve", Sst[l][gp, :], Sst[l][gp, :], bs[g][0][gp, :], ALU.add, [("S", l), bs[g][1]], [("S", l)])
                cp("act", Sbf[l][:], Sst[l][:], [("S", l)], [("Sb", l)])
                yield

            def tail(q):
                qs = slice(q * 128, (q + 1) * 128)
                zq = zsb[q % 2]
                xq = xs_tokb[q % 2]
                zkeys = [("zs", q % 2, 0), ("zs", q % 2, 1)]
                tt("pool", t2[:].rearrange("p (h d) -> p h d", h=16), xq[:].rearrange("p (h d) -> p h d", h=16),
                   bc(BP[:, 32:48], 2, 64), ALU.mult, [("xs_tok", q % 2), ("bp", l)], ["t2"])
                yield
                tt("dve", t1[:], t1[:], t2[:], ALU.add, [("t1", 0), ("t1", 1), "t2"], [("t1", 0), ("t1", 1)])
                yield
                tt("dve", t1[:], t1[:], zq[:], ALU.mult, [("t1", 0), ("t1", 1)] + zkeys, [("t1", 0), ("t1", 1)])
                yield
                act(t2[:], t1[:], AF.Square, [("t1", 0), ("t1", 1)], ["t2"])
                yield
                S.op("dve", lambda e: e.reduce_sum(out=ssq[:], in_=t2[:].rearrange("p (g d) -> p g d", g=2), axis=AX.X),
                     ["t2"], ["ssq"])
                cp("dve", rsv[:], ssq[:], ["ssq"], ["rsv"])
                ts("dve", rsv[:], rsv[:], 1.0 / 512, EPS, ALU.mult, ALU.add, ["rsv"], ["rsv"])
                yield
                act(rsv[:], rsv[:], AF.Ln, ["rsv"], ["rsv"])
                act(rsv[:], rsv[:], AF.Exp, ["rsv"], ["rsv"], scale=-0.5)
                yield
                for g in range(2):
                    hs = slice(g * 512, (g + 1) * 512)
                    ts("dve", vn[:, hs], t1[:, hs], rsv[:, g:g + 1], None, ALU.mult, None, [("t1", g), "rsv"], ["vn"])
                yield
                bk, bkey = bank()
                bkb = bk[:].bitcast(BF16)
                for c in range(8):
                    tr(bkb[:, c * 128:(c + 1) * 128], vn[:, c * 128:(c + 1) * 128], ["vn"], [bkey])
                tt("dve", bufA[:, :, qs], bkb.rearrange("p (c t) -> p c t", c=8), bc(V[:, VP["ng"]:VP["ng"] + 8], 2, 128),
                   ALU.mult, [bkey, ("vp", l)], Akeys)
                yield

            def run(g):
                for _ in g:
                    pass

            def zipper(a, b):
                a_live, b_live = True, True
                while a_live or b_live:
                    if a_live:
                        try:
                            next(a)
                        except StopIteration:
                            a_live = False
                    if b_live:
                        try:
                            next(b)
                        except StopIteration:
                            b_live = False

            def front_mid(q):
                yield from front(q)
                yield from mid(q)

            run(front_mid(0))
            for q in range(1, 4):
                zipper(tail(q - 1), front_mid(q))
            fill(1000)
            def conf_sc():
                cp("pool", halo_u[l][:], u_raw[:, :, T:T + 30], [("uraw", i) for i in range(4)], [("hu", l)])
                bm, bmk = bank()
                bq, bqk = bank()
                for i in range(4):
                    sq_, sqk = work()
                    act(sq_[:], cacc[:, i, :], AF.Square, [("cacc", i)], [sqk])
                    mm(bm[:], ones_f[:], cacc[:, i, :], i == 0, i == 3, ["const", ("cacc", i)], [bmk])
                    mm(bq[:], ones_f[:], sq_[:], i == 0, i == 3, ["const", sqk], [bqk])
                    yield
                mean, mk_ = work()
                msq, msk = work()
                rc, rck = work()
                ts("dve", mean[:], bm[:], 1.0 / 512, None, ALU.mult, None, [bmk], [mk_])
                tt("dve", msq[:], mean[:], mean[:], ALU.mult, [mk_], [msk])
                stt("dve", rc[:], bq[:], 1.0 / 512, msq[:], ALU.mult, ALU.subtract, [bqk, msk], [rck])
                rsqrt_inplace(rc[:], rck, 1.0)
                yield
                for i in range(4):
                    tt("dve", cacc[:, i, :], cacc[:, i, :], mean[:], ALU.subtract, [("cacc", i), mk_], [("cacc", i)])
                    tt("dve", cacc[:, i, :], cacc[:, i, :], rc[:], ALU.mult, [("cacc", i), rck], [("cacc", i)])
                    act(sbT[:, i, :], cacc[:, i, :], AF.Silu, [("cacc", i), ("vp", l)], [("sbT", i)],
                        bias=V[:, VP["lnb"] + i:VP["lnb"] + i + 1], scale=V[:, VP["lng"] + i:VP["lng"] + i + 1])
                    yield
                ssl, skey = wload([(dgs[l], 1, 12 * 128, 0, 12 * 128)], [("dgs", l)])
                for i in range(4):
                    bk, bkey = bank()
                    for k in range(3):
                        c_ = (i * 3 + k) * 128
                        mm(bk[:], ssl[:, c_:c_ + 128], p_raw[:, i, k:k + T], k == 0, k == 2,
                           [skey, ("praw", i), "praw_h"], [bkey])
                    tt("dve", q_bf[:, i, :], bk[:], gb_sb[:, i, :], ALU.mult, [bkey, ("gb", i)], [("q", i)])
                    yield
                cp("pool", halo_p[l][:], p_raw[:, :, T:T + 2], [("praw", i) for i in range(4)], [("hp", l)])
                yield

            zipper(tail(3), conf_sc())
            S.stage("merge")
            for qt in range(4):
                c0 = qt * 256
                g1, g1k = wload([(wi[:, OFF_G + c0:OFF_G + c0 + 256], 8, 512, 0, 256),
                                 (wi[:, OFF_G + 1024 + c0:OFF_G + 1024 + c0 + 256], 8, 512, 256, 256)], wbk)
                g2, g2k = wload([(wi[:, OFF_G + 2048 + c0:OFF_G + 2048 + c0 + 256], 8, 256, 0, 256)], wbk)
                g1v, g2v = wview(g1, 8, 512), wview(g2, 8, 256)
                for jj in range(2):
                    j = qt * 2 + jj
                    for wch, (wv_, wk_, co) in enumerate(((g1v, g1k, jj * 128), (g1v, g1k, 256 + jj * 128), (g2v, g2k, jj * 128))):
                        bk, bkey = proj_cm(wv_, wk_, co, 8, hB, hBk)
                        bcol = VP["bg"] + wch * 8 + j
                        act(sig[:, wch * 2 + jj, :], bk[:], AF.Sigmoid, [bkey, ("vp", l)], [("sig", wch * 2 + jj)],
                            bias=V[:, bcol:bcol + 1])
                wa, wak = wload([(wbf["w_ssd_out"][l][:, c0:c0 + 256], 8, 256, 0, 256)], [("wb", "w_ssd_out", l)])
                wb_, wbk2 = wload([(wbf["w_conf_out"][l][:, c0:c0 + 256], 4, 256, 0, 256)], [("wb", "w_conf_out", l)])
                wc_, wck = wload([(wbf["w_sc_out"][l][:, c0:c0 + 256], 4, 256, 0, 256)], [("wb", "w_sc_out", l)])
                wav, wbv, wcv = wview(wa, 8, 256), wview(wb_, 4, 256), wview(wc_, 4, 256)
                for jj in range(2):
                    j = qt * 2 + jj
                    ba, bak = proj_cm(wav, wak, jj * 128, 8, lambda kc: bufA[:, kc, :], lambda kc: [("bufA", kc)])
                    bb, bbk = proj_cm(wbv, wbk2, jj * 128, 4, lambda kc: sbT[:, kc, :], lambda kc: [("sbT", kc)])
                    bc_, bck = proj_cm(wcv, wck, jj * 128, 4, lambda kc: q_bf[:, kc, :], lambda kc: [("q", kc)])
                    m1, m1k = work()
                    m2, m2k = work()
                    m3, m3k = work()
                    tt("dve", m1[:], ba[:], sig[:, 0 + jj, :], ALU.mult, [bak, ("sig", 0 + jj)], [m1k])
                    tt("dve", m2[:], bb[:], sig[:, 2 + jj, :], ALU.mult, [bbk, ("sig", 2 + jj)], [m2k])
                    tt("dve", m3[:], bc_[:], sig[:, 4 + jj, :], ALU.mult, [bck, ("sig", 4 + jj)], [m3k])
                    tt("pool", m1[:], m1[:], m2[:], ALU.add, [m1k, m2k], [m1k])
                    tt("pool", xbcs[:, j, :], m1[:], m3[:], ALU.add, [m1k, m3k], [("xbcs", j)])
            S.stage("wo")
            gate = mod[(l, "mix")]
            for ob in range(2):
                wo, wok = wload([(wbf["w_o"][l][:, ob * 512:(ob + 1) * 512], 8, 512, 0, 512)], [("wb", "w_o", l)])
                wov = wview(wo, 8, 512)
                for jj in range(4):
                    j = ob * 4 + jj
                    bk, bkey = proj_cm(wov, wok, jj * 128, 8, lambda kc: xbcs[:, kc, :], lambda kc: [("xbcs", kc)])
                    stt("dve", x_sb[:, j, :], bk[:], gate[:, 16 + j:17 + j], x_sb[:, j, :], ALU.mult, ALU.add,
                        [bkey, ("mod", l, "mix"), ("x", j)], [("x", j)])

        def ffn(l):
            V = vp[l]
            wu = wbf["w_up"][l]
            S.stage("ffn_h")
            make_h(l, "ffn")
            S.stage("ffn_up")
            gate = mod[(l, "ffn")]
            TL = tails[l]
            abuf = [a_bf, a_bf2]
            for half in range(2):
                AB = abuf[half]
                for pb in range(6):
                    i0 = half * 11 + pb * 2
                    nch = min(2, half * 11 + 11 - i0)
                    ncol = nch * 128
                    slot, wkey = wload([(wu[:, i0 * 128:i0 * 128 + ncol], 8, 512, 0, ncol),
                                        (wu[:, DFF + i0 * 128:DFF + i0 * 128 + ncol], 8, 512, 256, ncol)],
                                       [("wb", "w_up", l)])
                    wv = wview(slot, 8, 512)
                    ch = []
                    for ii in range(nch):
                        i = i0 + ii
                        for (co, ci) in ((ii * 128, i), (256 + ii * 128, 22 + i)):
                            bk, bkey = proj_cm(wv, wkey, co, 8, hB, hBk)
                            acc, akey = work()
                            ch.append((bk, bkey, acc, akey, ci, VP["fw"] + ci * 3))
                    for (bk, bkey, acc, akey, ci, fw) in ch:
                        act(acc[:], bk[:], AF.Identity, [bkey, ("vp", l)], [akey],
                            bias=V[:, VP["fb"] + ci:VP["fb"] + ci + 1], scale=V[:, fw + 2:fw + 3])
                    for (bk, bkey, acc, akey, ci, fw) in ch:
                        stt("dve", acc[:, 1:T], bk[:, 0:T - 1], V[:, fw + 1:fw + 2], acc[:, 1:T], ALU.mult, ALU.add,
                            [bkey, ("vp", l), akey], [akey])
                    for (bk, bkey, acc, akey, ci, fw) in ch:
                        stt("dve", acc[:, 2:T], bk[:, 0:T - 2], V[:, fw:fw + 1], acc[:, 2:T], ALU.mult, ALU.add,
                            [bkey, ("vp", l), akey], [akey])
                    for (bk, bkey, acc, akey, ci, fw) in ch:
                        stt("dve", acc[:, 0:2], TL[:, ci, 0:2], V[:, fw:fw + 1], acc[:, 0:2], ALU.mult, ALU.add,
                            [("tl", l, ci), ("tl", l), ("vp", l), akey], [akey])
                    for (bk, bkey, acc, akey, ci, fw) in ch:
                        stt("dve", acc[:, 0:1], TL[:, ci, 1:2], V[:, fw + 1:fw + 2], acc[:, 0:1], ALU.mult, ALU.add,
                            [("tl", l, ci), ("tl", l), ("vp", l), akey], [akey])
                    for (bk, bkey, acc, akey, ci, fw) in ch:
                        cp("act", TL[:, ci, :], bk[:, T - 2:T], [bkey, ("tl", l)], [("tl", l, ci)])
                    for ii in range(nch):
                        ag, agk = ch[2 * ii][2], ch[2 * ii][3]
                        act(ag[:], ag[:], AF.Silu, [agk], [agk])
                    for ii in range(nch):
                        i = i0 + ii
                        ag, agk = ch[2 * ii][2], ch[2 * ii][3]
                        av, avk = ch[2 * ii + 1][2], ch[2 * ii + 1][3]
                        tt("dve", AB[:, i - half * 11, :], av[:], ag[:], ALU.mult, [agk, avk], [("a_bf", half, i - half * 11)])
            wdn = wbf["w_down"][l]
            for half in range(2):
                AB = abuf[half]
                for jb in range(4):
                    slot, wkey = wload([(wdn[half * 11 * 128:(half + 1) * 11 * 128, jb * 256:(jb + 1) * 256], 11, 256, 0, 256)],
                                       [("wb", "w_down", l)])
                    wv = wview(slot, 11, 256)
                    for jj in range(2):
                        j = jb * 2 + jj
                        bk, bkey = proj_cm(wv, wkey, jj * 128, 11, (lambda AB_: (lambda kc: AB_[:, kc, :]))(AB),
                                           (lambda h_: (lambda kc: [("a_bf", h_, kc)]))(half))
                        stt("dve", x_sb[:, j, :], bk[:], gate[:, 16 + j:17 + j], x_sb[:, j, :], ALU.mult, ALU.add,
                            [bkey, ("mod", l, "ffn"), ("x", j)], [("x", j)])

        xT3 = xT.rearrange("(k p) t -> p k t", p=128)
        oT3 = outT.rearrange("(k p) t -> p k t", p=128)
        chx = [S.chan("x%d" % kc) for kc in range(8)]
        cho = [S.chan("o%d" % i) for i in range(NWK)]

        def load_x(ti):
            t0 = ti * T
            for kc in range(8):
                S.op("sp", (lambda d, s_: (lambda e: e.dma_start(out=d, in_=s_)))(x_sb[:, kc, :], xT3[:, kc, t0:t0 + T]),
                     [], [("x", kc)], dma=chx[kc], force=True)

        load_x(0)
        for ti in range(NT):
            t0 = ti * T
            for l in range(depth):
                mixer(l)
                ffn(l)
            S.stage("final")
            rms(depth - 1)
            gf = VP["gfin"]
            for kc in range(8):
                wt_, wkk = work()
                stt("dve", wt_[:], x_sb[:, kc, :], vp[0][:, gf + kc:gf + kc + 1], rstd[:], ALU.mult, ALU.mult,
                    [("x", kc), ("vp", 0), "rstd"], [wkk])
                S.op("sp", (lambda d, s_: (lambda e: e.dma_start(out=d, in_=s_)))(oT3[:, kc, t0:t0 + T], wt_[:]),
                     [wkk], [("out", ti, kc)], dma=cho[wkk[1]], force=True)
                if ti + 1 < NT:
                    t1_ = (ti + 1) * T
                    S.op("sp", (lambda d, s_: (lambda e: e.dma_start(out=d, in_=s_)))(x_sb[:, kc, :], xT3[:, kc, t1_:t1_ + T]),
                         [], [("x", kc)], dma=chx[kc], force=True)
        S.final_waits("sp")

        with nc.Block() as block:
            @block.tensor
            def _(e):
                S.emit("pe", e)

            @block.scalar
            def _(e):
                S.emit("act", e)

            @block.vector
            def _(e):
                S.emit("dve", e)

            @block.gpsimd
            def _(e):
                S.emit("pool", e)

            @block.sync
            def _(e):
                S.emit("sp", e)
    return nc


def make_packs(inp, depth=DEPTH):
    f = lambda a: np.asarray(a, dtype=np.float32)

    def cm(v):
        v = f(v)
        return v.reshape(-1, 128).T

    def cmk(w):
        w = f(w)
        K, C = w.shape
        return w.T.reshape(C // 128, 128, K).transpose(1, 0, 2).reshape(128, -1)

    vp = np.zeros((depth, 128, NV), np.float32)
    bp = np.zeros((depth, 128, NB), np.float32)
    for l in range(depth):
        def put(name, arr):
            vp[l, :, VP[name]:VP[name] + arr.shape[1]] = arr
        put("g_mix", cm(inp["norm_mix_g"][l]))
        put("g_ffn", cm(inp["norm_ffn_g"][l]))
        put("ab_mix", cm(inp["ada_mix_b"][l]))
        put("ab_ffn", cm(inp["ada_ffn_b"][l]))
        put("xw", cmk(inp["ssd_conv_w"][l]))
        put("xb", cm(inp["ssd_conv_b"][l]))
        put("cw", cmk(inp["conf_conv_w"][l]))
        put("cb", cm(inp["conf_conv_b"][l]))
        put("lng", cm(inp["conf_ln_g"][l]))
        put("lnb", cm(inp["conf_ln_b"][l]))
        put("sw", cmk(inp["sc_conv_w"][l]))
        put("bg", cm(inp["b_gate"][l]))
        put("ng", cm(inp["ssd_norm_g"][l]))
        put("fw", cmk(inp["ffn_conv_w"][l]))
        put("fb", cm(inp["ffn_conv_b"][l]))
        put("gfin", cm(inp["final_norm_g"]))
        bp[l, :, 0:16] = f(inp["ssd_dt_bias"][l])[None, :]
        bp[l, :, 16:32] = f(inp["ssd_a_log"][l])[None, :]
        bp[l, :, 32:48] = f(inp["ssd_d"][l])[None, :]
    return vp, bp


_NC_CACHE = {}


def kernel(**inputs):
    x = np.asarray(inputs["x"], dtype=np.float32)
    B, L, _ = x.shape
    depth = np.asarray(inputs["w_in"]).shape[0]
    key = (L, depth)
    if key not in _NC_CACHE:
        _NC_CACHE[key] = build_nc(L, depth)
    nc = _NC_CACHE[key]
    vp, bp = make_packs(inputs, depth)
    c = np.asarray(inputs["c"], dtype=np.float32)
    wnames = ["ada_mix_w", "ada_ffn_w", "w_in", "w_ssd_out", "w_conf_out", "w_sc_out", "w_o", "w_up", "w_down"]
    wts_ = {n: np.ascontiguousarray(np.asarray(inputs[n], dtype=np.float32)) for n in wnames}
    n_cores = int(os.environ.get("KCORES", "8"))
    in_maps = []
    active = {0: 0, 1: 1, 4: 2, 5: 3} if (n_cores == 8 and B == 4) else {i: (i * B) // n_cores for i in range(n_cores)}
    zero_w = None
    for i in range(n_cores):
        if i in active:
            b = active[i]
            m = {"xT": np.ascontiguousarray(x[b].T), "cpk": np.ascontiguousarray(c[b].reshape(8, 128).T),
                 "vp": vp, "bp": bp}
            m.update(wts_)
        else:
            if zero_w is None:
                zero_w = {n: np.zeros_like(a) for n, a in wts_.items()}
                zero_w.update({"xT": np.zeros((D, L), np.float32), "cpk": np.zeros((128, 8), np.float32),
                               "vp": np.zeros_like(vp), "bp": np.zeros_like(bp)})
            m = dict(zero_w)
        in_maps.append(m)
    res = run_bass_kernel_spmd(nc, in_maps, core_ids=list(range(n_cores)))
    out = np.empty((B, L, D), np.float32)
    inv = {}
    for i, b in active.items():
        inv.setdefault(b, i)
    for b in range(B):
        i = inv.get(b, 0)
        out[b] = np.asarray(res.results[i]["outT"]).T
    return out
```
